# Optimizing a Trainium2 kernel written in Bass

```python
import math
import jax, jax.numpy as jnp
from jax import lax
import numpy as np

D_MODEL = 1024
BATCH = 8
SEQ = 4096
DEPTH = 1

SSM_GROUPS = 32
SSM_GROUP_CH = 16
SSM_WIDTH = SSM_GROUPS * SSM_GROUP_CH
SSM_STATE = 64
DT_MIN = 1e-3
DT_MAX = 1e-1
N_HEADS = 8
QK_NOPE = 128
QK_ROPE = 64
QK_HEAD = QK_NOPE + QK_ROPE
V_HEAD = 128
Q_LORA = 384
KV_LORA = 256
ROPE_THETA = 10000.0
Q_BLOCK = 128
MAX_POS_OFFSET = 1024
D_FF = 4 * D_MODEL
EPS = 1e-6
IN_SIZES = (SSM_WIDTH, Q_LORA, KV_LORA + QK_ROPE, D_MODEL, D_MODEL)
IN_OFFSETS = tuple(int(v) for v in np.cumsum(IN_SIZES)[:-1])
D_IN = sum(IN_SIZES)

kernel_name = "hybrid_s5_mla_gated_block"


def rms_norm(x, gain):
    xf = x.astype(jnp.float32)
    inv = lax.rsqrt(jnp.mean(xf * xf, axis=-1, keepdims=True) + EPS)
    return (xf * inv * gain.astype(jnp.float32)).astype(x.dtype)


def rope_tables(positions):
    half = QK_ROPE // 2
    inv_freq = ROPE_THETA ** (-jnp.arange(half, dtype=jnp.float32) / half)
    ang = positions.astype(jnp.float32)[..., None] * inv_freq
    return jnp.cos(ang)[:, :, None, :], jnp.sin(ang)[:, :, None, :]


def apply_rope(x, cos, sin):
    xf = x.astype(jnp.float32)
    x1, x2 = jnp.split(xf, 2, axis=-1)
    out = jnp.concatenate([x1 * cos - x2 * sin, x2 * cos + x1 * sin], axis=-1)
    return out.astype(x.dtype)


def causal_block_attention(q, k, v):
    b, l, h, dq = q.shape
    nblk = l // Q_BLOCK
    qb = q.reshape(b, nblk, Q_BLOCK, h, dq).transpose(1, 0, 2, 3, 4)
    key_idx = jnp.arange(l)
    scale = QK_HEAD ** -0.5

    def one_block(args):
        q_blk, blk = args
        s = jnp.einsum('bqhd,bkhd->bhqk', q_blk, k, preferred_element_type=jnp.float32) * scale
        q_idx = blk * Q_BLOCK + jnp.arange(Q_BLOCK)
        s = jnp.where(key_idx[None, :] <= q_idx[:, None], s, -jnp.inf)
        p = jax.nn.softmax(s, axis=-1).astype(v.dtype)
        return jnp.einsum('bhqk,bkhd->bqhd', p, v)

    out = lax.map(one_block, (qb, jnp.arange(nblk)))
    return out.transpose(1, 0, 2, 3, 4).reshape(b, l, h, -1)


def s5_ssm(u, a_re, a_im, log_dt, b_re, b_im, c_re, c_im, d_skip):
    f32 = jnp.float32
    dt = jnp.exp(log_dt.astype(f32))[:, None]
    lam = lax.complex(a_re.astype(f32), a_im.astype(f32))
    lam_bar = jnp.exp(lam * dt)
    b_mat = lax.complex(b_re.astype(f32), b_im.astype(f32))
    b_bar = ((lam_bar - 1.0) / lam)[..., None] * b_mat
    bu = jnp.einsum('gpc,blgc->blgp', b_bar, u.astype(f32).astype(jnp.complex64))
    a = jnp.broadcast_to(lam_bar, (1, u.shape[1]) + lam_bar.shape)

    def combine(left, right):
        a_l, b_l = left
        a_r, b_r = right
        return a_r * a_l, a_r * b_l + b_r

    _, states = lax.associative_scan(combine, (a, bu), axis=1)
    c_mat = lax.complex(c_re.astype(f32), c_im.astype(f32))
    y = jnp.real(jnp.einsum('gcp,blgp->blgc', c_mat, states)) + d_skip.astype(f32) * u.astype(f32)
    return y.astype(u.dtype)


def hybrid_layer(x, positions, norm_mix, w_in, q_a_norm, kv_a_norm, w_q_b, w_kv_b, q_norm, k_norm, w_o_mla,
                 ssm_a_re, ssm_a_im, ssm_log_dt, ssm_b_re, ssm_b_im, ssm_c_re, ssm_c_im, ssm_d,
                 w_glu, b_glu, w_o_ssm, w_out, norm_mlp, w_up, w_down):
    b, l, _ = x.shape
    xn = rms_norm(x, norm_mix)
    proj = xn @ w_in
    u, q_lat, kv_lat, gate_ssm, gate_mla = jnp.split(proj, IN_OFFSETS, axis=-1)

    y = s5_ssm(u.reshape(b, l, SSM_GROUPS, SSM_GROUP_CH), ssm_a_re, ssm_a_im, ssm_log_dt,
               ssm_b_re, ssm_b_im, ssm_c_re, ssm_c_im, ssm_d).reshape(b, l, SSM_WIDTH)
    z = jax.nn.gelu(y)
    z = z * jax.nn.sigmoid(z @ w_glu + b_glu)
    y_ssm = z @ w_o_ssm

    q = (rms_norm(q_lat, q_a_norm) @ w_q_b).reshape(b, l, N_HEADS, QK_HEAD)
    c_kv, k_pe = kv_lat[..., :KV_LORA], kv_lat[..., KV_LORA:]
    kv = (rms_norm(c_kv, kv_a_norm) @ w_kv_b).reshape(b, l, N_HEADS, QK_NOPE + V_HEAD)
    k_nope, v = kv[..., :QK_NOPE], kv[..., QK_NOPE:]
    k = jnp.concatenate([k_nope, jnp.broadcast_to(k_pe[:, :, None, :], (b, l, N_HEADS, QK_ROPE))], axis=-1)
    q = rms_norm(q, q_norm)
    k = rms_norm(k, k_norm)
    cos, sin = rope_tables(positions)
    q = jnp.concatenate([q[..., :QK_NOPE], apply_rope(q[..., QK_NOPE:], cos, sin)], axis=-1)
    k = jnp.concatenate([k[..., :QK_NOPE], apply_rope(k[..., QK_NOPE:], cos, sin)], axis=-1)
    attn = causal_block_attention(q, k, v).reshape(b, l, N_HEADS * V_HEAD)
    y_mla = attn @ w_o_mla

    mixed = jax.nn.sigmoid(gate_ssm) * y_ssm + jax.nn.sigmoid(gate_mla) * y_mla
    h = x + mixed @ w_out

    hidden = jnp.square(jax.nn.relu(rms_norm(h, norm_mlp) @ w_up))
    return h + hidden @ w_down


def setup_inputs(seed: int = 0) -> dict:
    key = jax.random.key(seed)
    ks = jax.random.split(key, 32)
    f32 = jnp.float32

    def dense(k, fan_in, shape):
        return jax.random.normal(k, (DEPTH,) + shape, f32) * (fan_in ** -0.5)

    def gain(k, n):
        return 1.0 + 0.02 * jax.random.normal(k, (DEPTH, n), f32)

    x = jax.random.normal(ks[0], (BATCH, SEQ, D_MODEL), f32)
    offset = jax.random.randint(ks[1], (BATCH, 1), 0, MAX_POS_OFFSET, dtype=jnp.int32)
    positions = (offset + jnp.arange(SEQ, dtype=jnp.int32)[None, :]).astype(jnp.int32)
    n_idx = jnp.arange(SSM_STATE, dtype=f32)
    ssm_a_re = -0.5 + 0.01 * jax.random.normal(ks[2], (DEPTH, SSM_GROUPS, SSM_STATE), f32)
    ssm_a_im = math.pi * n_idx[None, None, :] + 0.01 * jax.random.normal(ks[3], (DEPTH, SSM_GROUPS, SSM_STATE), f32)
    ssm_log_dt = jax.random.uniform(ks[4], (DEPTH, SSM_GROUPS), f32, math.log(DT_MIN), math.log(DT_MAX))
    return {
        "x": x,
        "positions": positions,
        "norm_mix": gain(ks[5], D_MODEL),
        "w_in": dense(ks[6], D_MODEL, (D_MODEL, D_IN)),
        "q_a_norm": gain(ks[7], Q_LORA),
        "kv_a_norm": gain(ks[8], KV_LORA),
        "w_q_b": dense(ks[9], Q_LORA, (Q_LORA, N_HEADS * QK_HEAD)),
        "w_kv_b": dense(ks[10], KV_LORA, (KV_LORA, N_HEADS * (QK_NOPE + V_HEAD))),
        "q_norm": gain(ks[11], QK_HEAD),
        "k_norm": gain(ks[12], QK_HEAD),
        "w_o_mla": dense(ks[13], N_HEADS * V_HEAD, (N_HEADS * V_HEAD, D_MODEL)),
        "ssm_a_re": ssm_a_re,
        "ssm_a_im": ssm_a_im,
        "ssm_log_dt": ssm_log_dt,
        "ssm_b_re": dense(ks[14], 2 * SSM_GROUP_CH, (SSM_GROUPS, SSM_STATE, SSM_GROUP_CH)),
        "ssm_b_im": dense(ks[15], 2 * SSM_GROUP_CH, (SSM_GROUPS, SSM_STATE, SSM_GROUP_CH)),
        "ssm_c_re": dense(ks[16], 2 * SSM_STATE, (SSM_GROUPS, SSM_GROUP_CH, SSM_STATE)),
        "ssm_c_im": dense(ks[17], 2 * SSM_STATE, (SSM_GROUPS, SSM_GROUP_CH, SSM_STATE)),
        "ssm_d": jax.random.normal(ks[18], (DEPTH, SSM_GROUPS, SSM_GROUP_CH), f32),
        "w_glu": dense(ks[19], SSM_WIDTH, (SSM_WIDTH, SSM_WIDTH)),
        "b_glu": 0.01 * jax.random.normal(ks[20], (DEPTH, SSM_WIDTH), f32),
        "w_o_ssm": dense(ks[21], SSM_WIDTH, (SSM_WIDTH, D_MODEL)),
        "w_out": dense(ks[22], D_MODEL, (D_MODEL, D_MODEL)),
        "norm_mlp": gain(ks[23], D_MODEL),
        "w_up": dense(ks[24], D_MODEL, (D_MODEL, D_FF)),
        "w_down": dense(ks[25], D_FF, (D_FF, D_MODEL)),
    }


def reference(x, positions, norm_mix, w_in, q_a_norm, kv_a_norm, w_q_b, w_kv_b, q_norm, k_norm, w_o_mla,
              ssm_a_re, ssm_a_im, ssm_log_dt, ssm_b_re, ssm_b_im, ssm_c_re, ssm_c_im, ssm_d,
              w_glu, b_glu, w_o_ssm, w_out, norm_mlp, w_up, w_down):
    h = x
    for layer in range(DEPTH):
        h = hybrid_layer(h, positions, norm_mix[layer], w_in[layer], q_a_norm[layer], kv_a_norm[layer],
                         w_q_b[layer], w_kv_b[layer], q_norm[layer], k_norm[layer], w_o_mla[layer],
                         ssm_a_re[layer], ssm_a_im[layer], ssm_log_dt[layer], ssm_b_re[layer], ssm_b_im[layer],
                         ssm_c_re[layer], ssm_c_im[layer], ssm_d[layer], w_glu[layer], b_glu[layer],
                         w_o_ssm[layer], w_out[layer], norm_mlp[layer], w_up[layer], w_down[layer])
    return h
```

```python
import bisect
import contextlib
import math
import numpy as np
import concourse.bass as bass
import concourse.mybir as mybir
from concourse.bass_utils import run_bass_kernel_spmd

F32 = mybir.dt.float32
BF16 = mybir.dt.bfloat16
I32 = mybir.dt.int32
AF = mybir.ActivationFunctionType
ALU = mybir.AluOpType

EPS = 1e-6
D = 1024
DIN = 3264
PI = math.pi
ENGINES = ("tensor", "vector", "scalar", "gpsimd", "sync")


class Sems:
    def __init__(self, nc, stack, n_dma=56):
        self.esem = {e: stack.enter_context(nc.semaphore(f"s_{e}")) for e in ENGINES}
        self.bar = stack.enter_context(nc.semaphore("s_bar"))
        self.pool = {q: [stack.enter_context(nc.semaphore(f"s_{q}{i}")) for i in range(n)]
                     for q, n in (("hw", n_dma), ("sw", 24))}
        self.eng_cnt = {e: 0 for e in ENGINES}
        self.pool_cnt = {q: [0] * len(v) for q, v in self.pool.items()}
        self.phase_no = 0


class Phase:
    def __init__(self, nc, name, G):
        self.nc = nc
        self.G = G
        self.name = name
        self.ops = []
        self.stack = contextlib.ExitStack()

    PSUM_KEYS = ("bank", "pb", "st", "ob", "obT", "tp", "ps", "ub", "yb", "oT", "dn", "facc")

    def is_psum(self, key):
        k0 = key[0] if isinstance(key, tuple) else key
        return isinstance(k0, str) and (k0 in self.PSUM_KEYS or k0.startswith("tp"))

    def sb(self, name, shape, dtype):
        return self.stack.enter_context(self.nc.sbuf_tensor(f"{self.name}_{name}", list(shape), dtype))

    def ps(self, name, shape, dtype=F32):
        return self.stack.enter_context(self.nc.psum_tensor(f"{self.name}_{name}", list(shape), dtype))

    def __enter__(self):
        self.stack.__enter__()
        return self

    def op(self, eng, name, reads=(), writes=(), **kw):
        self.ops.append(dict(eng=eng, fn=(lambda e, name=name, kw=kw: getattr(e, name)(**kw)),
                             reads=tuple(reads), writes=tuple(writes), dma=None))

    def mm(self, out, lhsT, rhs, start, stop, reads, writes):
        self.op("tensor", "matmul", reads, writes, out=out, lhsT=lhsT, rhs=rhs, start=start, stop=stop)

    def tr(self, out, in_, identity, reads, writes):
        self.op("tensor", "transpose", reads, writes, out=out, in_=in_, identity=identity)

    def act(self, out, in_, func, reads, writes, **kw):
        self.op("scalar", "activation", reads, writes, out=out, in_=in_, func=func, **kw)

    def copy(self, eng, out, in_, reads, writes):
        if eng == "scalar":
            self.op("scalar", "copy", reads, writes, out=out, in_=in_)
        else:
            self.op(eng, "tensor_copy", reads, writes, out=out, in_=in_)

    def tt(self, eng, out, in0, in1, op, reads, writes):
        self.op(eng, "tensor_tensor", reads, writes, out=out, in0=in0, in1=in1, op=op)

    def ts(self, eng, out, in0, s1, s2, op0, op1, reads, writes):
        if s2 is None:
            self.op(eng, "tensor_scalar", reads, writes, out=out, in0=in0, scalar1=s1, scalar2=None, op0=op0)
        else:
            self.op(eng, "tensor_scalar", reads, writes, out=out, in0=in0, scalar1=s1, scalar2=s2, op0=op0, op1=op1)

    def stt(self, out, in0, scalar, in1, op0, op1, reads, writes):
        self.op("vector", "scalar_tensor_tensor", reads, writes, out=out, in0=in0, scalar=scalar, in1=in1, op0=op0, op1=op1)

    def dma(self, eng, out, in_, reads=(), writes=(), key=None, **kw):
        assert key is not None
        self.ops.append(dict(eng=eng, fn=(lambda e, out=out, in_=in_, kw=kw: e.dma_start(out=out, in_=in_, **kw)),
                             reads=tuple(reads), writes=tuple(writes), dma=key))

    def __exit__(self, et, ev, tb):
        if et is None:
            self._emit()
        return self.stack.__exit__(et, ev, tb)

    def _emit(self):
        nc = self.nc
        ops = self.ops
        n = len(ops)
        last_w, readers = {}, {}
        deps = [None] * n
        for i, o in enumerate(ops):
            d = set()
            raw = set()
            for r in o["reads"]:
                if r in last_w:
                    d.add(last_w[r])
                    raw.add(last_w[r])
            weff = list(o["writes"]) + [r for r in o["reads"] if self.is_psum(r) and r not in o["writes"]]
            for w in weff:
                if w in last_w:
                    d.add(last_w[w])
                d.update(readers.get(w, ()))
            d.discard(i)
            d = {j for j in d if ops[j]["dma"] is not None or ops[j]["eng"] != o["eng"]
                 or o["eng"] != "tensor" or o["dma"] is not None}
            deps[i] = d
            for w in weff:
                last_w[w] = i
                readers[w] = []
            for r in o["reads"]:
                if r not in weff:
                    readers.setdefault(r, []).append(i)
        need_sig = [False] * n
        for i in range(n):
            for j in deps[i]:
                if ops[j]["dma"] is None:
                    need_sig[j] = True
        G = self.G
        eng_cnt = dict(G.eng_cnt)
        sig_val = [0] * n
        dma_idx = {}
        for i, o in enumerate(ops):
            if o["dma"] is not None:
                dma_idx.setdefault(o["dma"], []).append(i)
            elif need_sig[i]:
                eng_cnt[o["eng"]] += 1
                sig_val[i] = eng_cnt[o["eng"]]
        keys = list(dma_idx)
        kq = {}
        for k in keys:
            qs = {"sw" if ops[i]["eng"] == "gpsimd" else "hw" for i in dma_idx[k]}
            assert len(qs) == 1, (k, qs)
            kq[k] = qs.pop()
        kslot, nq = {}, {"hw": 0, "sw": 0}
        for k in keys:
            kslot[k] = nq[kq[k]]
            nq[kq[k]] += 1
            assert nq[kq[k]] <= len(G.pool[kq[k]]), (self.name, kq[k], nq)
        ksem = {k: G.pool[kq[k]][kslot[k]] for k in keys}
        kbase = {k: G.pool_cnt[kq[k]][kslot[k]] for k in keys}
        G.phase_no += 1
        phase_no = G.phase_no
        per_eng = {e: [] for e in ENGINES}
        for i, o in enumerate(ops):
            per_eng[o["eng"]].append(i)
        with nc.Block() as block:

            def make(e_name):
                idxs = per_eng[e_name]

                def body(e):
                    waited = {}
                    for i in idxs:
                        o = ops[i]
                        want = {}
                        for j in deps[i]:
                            pj = ops[j]
                            if pj["dma"] is not None:
                                k = pj["dma"]
                                sk = ("d", k)
                                v = kbase[k] + 16 * bisect.bisect_left(dma_idx[k], i)
                            else:
                                sk = ("e", pj["eng"])
                                v = sig_val[j]
                            want[sk] = max(want.get(sk, 0), v)
                        for sk, v in sorted(want.items(), key=lambda kv: str(kv[0])):
                            if waited.get(sk, 0) >= v:
                                continue
                            waited[sk] = v
                            e.wait_ge(ksem[sk[1]] if sk[0] == "d" else G.esem[sk[1]], v)
                        ins = o["fn"](e)
                        if o["dma"] is not None:
                            ins.then_inc(ksem[o["dma"]], 16)
                        elif need_sig[i]:
                            ins.then_inc(G.esem[e_name], 1)
                    for k in sorted({ops[i]["dma"] for i in idxs if ops[i]["dma"] is not None}, key=str):
                        e.wait_ge(ksem[k], kbase[k] + 16 * len(dma_idx[k]))
                    e.sem_inc(G.bar, 1)
                    e.wait_ge(G.bar, len(ENGINES) * phase_no)
                return body

            for e_name in ENGINES:
                getattr(block, e_name)(make(e_name))
        G.eng_cnt = eng_cnt
        for k in keys:
            G.pool_cnt[kq[k]][kslot[k]] = kbase[k] + 16 * len(dma_idx[k])


class RR:
    def __init__(self, items):
        self.items = list(items)
        self.i = 0

    def __call__(self):
        v = self.items[self.i % len(self.items)]
        self.i += 1
        return v


def range_reduce_sin(P, eng, ang, out, tmp_i, tmp_f, key, out_key, shift=0.0):
    ka, ki, kf = key + "_ang", key + "_ti", key + "_tf"
    P.ts(eng, tmp_i, ang, 1.0 / (2 * PI), shift / (2 * PI), ALU.mult, ALU.add, [ka], [ki])
    P.copy(eng, tmp_f, tmp_i, [ki], [kf])
    P.ts(eng, tmp_f, tmp_f, -2 * PI, shift, ALU.mult, ALU.add, [kf], [kf])
    P.tt(eng, tmp_f, tmp_f, ang, ALU.add, [kf, ka], [kf])
    P.ts(eng, tmp_f, tmp_f, PI, -PI, ALU.min, ALU.max, [kf], [kf])
    P.act(out, tmp_f, AF.Sin, [kf], [out_key])


VAR = 0


def build(L=4096, debug=(), stop_after=None, pad=0, amode=2, pstage=9, pcut=99):
    assert L % 1024 == 0
    NB = L // 512
    NS = L // 1024
    NC = L // 8
    nc = bass.Bass("TRN2", target_bir_lowering=False)
    dbg = {}

    def din(name, shape, dt=F32):
        return nc.dram_tensor(name, list(shape), dt, kind="ExternalInput").ap()

    x = din("x", [L, D])
    pos = din("positions", [L], I32)
    norm_mix = din("norm_mix", [D])
    w_in = din("w_in", [D, DIN])
    q_a_norm = din("q_a_norm", [384])
    kv_a_norm = din("kv_a_norm", [256])
    w_q_b = din("w_q_b", [384, 1536])
    w_kv_b = din("w_kv_b", [256, 2048])
    q_norm = din("q_norm", [192])
    k_norm = din("k_norm", [192])
    w_o_mla = din("w_o_mla", [1024, 1024])
    ssm_a_re = din("ssm_a_re", [32, 64])
    ssm_a_im = din("ssm_a_im", [32, 64])
    ssm_log_dt = din("ssm_log_dt", [32])
    ssm_b_re = din("ssm_b_re", [32, 64, 16])
    ssm_b_im = din("ssm_b_im", [32, 64, 16])
    ssm_c_re = din("ssm_c_re", [32, 16, 64])
    ssm_c_im = din("ssm_c_im", [32, 16, 64])
    ssm_d = din("ssm_d", [32, 16])
    w_glu = din("w_glu", [512, 512])
    b_glu = din("b_glu", [512])
    w_o_ssm = din("w_o_ssm", [512, 1024])
    w_out = din("w_out", [1024, 1024])
    norm_mlp = din("norm_mlp", [1024])
    w_up = din("w_up", [1024, 4096])
    w_down = din("w_down", [4096, 1024])
    inv_freq2 = din("inv_freq2", [64, 1])
    out = nc.dram_tensor("out", [L, D], F32, kind="ExternalOutput").ap()

    def dbg_out(name, shape, dt=F32):
        t = nc.dram_tensor("dbg_" + name, list(shape), dt, kind="ExternalOutput").ap()
        dbg[name] = t
        return t

    def scratch(name, shape, dt):
        if name in debug:
            return dbg_out(name, shape, dt)
        return nc.dram_tensor("s_" + name, list(shape), dt).ap()

    gates_s = scratch("gates", [2048, L], BF16)
    qln_s = scratch("qln", [384, L], BF16)
    ckvn_s = scratch("ckvn", [256, L], BF16)
    kper_s = scratch("kper", [64, L], BF16)
    sskpe_s = scratch("sskpe", [128, L], F32)
    rope_s = scratch("rope", [2, 64, L], F32)
    attnT_s = scratch("attnT", [1024, L], BF16)
    gluT_s = scratch("gluT", [512, L], BF16)
    h_s = scratch("h", [L, D], F32)
    u8_s = scratch("u8", [128, 32, NC], BF16)

    def cast_load(P, dst_tile, src, rows, key, wkey):
        per = (rows + 3) // 4
        for k in range(rows):
            P.dma("gpsimd", dst_tile[:, k, :], src[k * 128:(k + 1) * 128, :], reads=[("cseq", wkey, k - 2)] if k >= 2 else [],
                  writes=[(wkey, k), ("cseq", wkey, k)], key=f"{key}{k // per}", max_dma_last_dim=4096)

    def make_ident(P, identf, ident):
        P.op("gpsimd", "memset", [], ["identf"], ap=identf[:], constant=1.0)
        P.op("gpsimd", "affine_select", ["identf"], ["identf"], out=identf[:], in_=identf[:], pattern=[[-1, 128]],
             compare_op=ALU.is_equal, fill=0.0, base=0, channel_multiplier=1)
        P.copy("gpsimd", ident[:], identf[:], ["identf"], ["ident"])

    def make_rm(P, rm, rm2):
        P.op("gpsimd", "memset", [], ["rm"], ap=rm[:], constant=-1.0)
        P.op("gpsimd", "affine_select", ["rm"], ["rm"], out=rm[:], in_=rm[:], pattern=[[-1, 64]], compare_op=ALU.is_equal,
             fill=0.0, base=-32, channel_multiplier=1)
        P.op("gpsimd", "memset", [], ["rm2"], ap=rm2[:], constant=1.0)
        P.op("gpsimd", "affine_select", ["rm2"], ["rm2"], out=rm2[:], in_=rm2[:], pattern=[[-1, 64]], compare_op=ALU.is_equal,
             fill=0.0, base=32, channel_multiplier=1)
        P.tt("gpsimd", rm[:], rm[:], rm2[:], ALU.add, ["rm", "rm2"], ["rm"])

    gstack = contextlib.ExitStack()
    G = Sems(nc, gstack)

    with Phase(nc, "p1", G) as P:
        if pad:
            P.sb("pad", [128, pad], F32)
        win = P.sb("win", [128, 8, DIN], BF16)
        WCH = [(512, 1216), (1216, 2240), (2240, 3264), (0, 512)]
        seq = 0
        for c, (a0, a1) in enumerate(WCH):
            for k in range(8):
                P.dma("gpsimd", win[:, k, a0:a1], w_in[k * 128:(k + 1) * 128, a0:a1],
                      reads=[("cseq", seq - 2)] if seq >= 2 else [], writes=[("win", k, c), ("cseq", seq)], key=f"win{c}", max_dma_last_dim=4096)
                seq += 1

        def wkey(k, col0):
            for c, (a0, a1) in enumerate(WCH):
                if a0 <= col0 < a1:
                    return ("win", k, c)
        gmix = P.sb("gmix", [128, D], F32)
        P.dma("sync", gmix[:], norm_mix.partition_broadcast(128), writes=["gmix"], key="gmix")
        gq = P.sb("gq", [128, 5, 1], F32)
        P.dma("sync", gq[:, 0:3, :], bass.AP(q_a_norm.tensor, 0, [[1, 128], [128, 3], [1, 1]]), writes=["gq"], key="gq", allow_slow_non_contiguous=True)
        P.dma("sync", gq[:, 3:5, :], bass.AP(kv_a_norm.tensor, 0, [[1, 128], [128, 2], [1, 1]]), writes=["gq"], key="gq", allow_slow_non_contiguous=True)
        gkr = P.sb("gkr", [64, 1], F32)
        P.dma("sync", gkr[:], bass.AP(k_norm.tensor, 128, [[1, 64], [1, 1]]), writes=["gkr"], key="gkr")
        invf = P.sb("invf", [64, 1], F32)
        P.dma("sync", invf[:], inv_freq2, writes=["invf"], key="invf")
        identf = P.sb("identf", [128, 128], F32)
        ident = P.sb("ident", [128, 128], BF16)
        ones_bf = P.sb("ones", [128, 128], BF16)
        make_ident(P, identf, ident)
        P.op("gpsimd", "memset", [], ["ones"], ap=ones_bf[:], constant=1.0)
        rm = P.sb("rm", [64, 64], F32)
        rm2 = P.sb("rm2", [64, 64], F32)
        make_rm(P, rm, rm2)

        xs = [P.sb(f"xs{i}", [128, 4, D], F32) for i in range(2)]
        xn = [P.sb(f"xn{i}", [128, 4, D], BF16) for i in range(1)]
        xnT = [P.sb(f"xnT{i}", [128, 8, 1024], BF16) for i in range(2)]
        junk = [P.sb(f"junk{i}", [128, D], BF16) for i in range(4)]
        ssx = P.sb("ssx", [128, 4 * NB], F32)
        inv = P.sb("inv", [128, 4 * NB], F32)
        u8 = P.sb("u8", [128, 32, 8, 16], BF16)
        u8T = [P.sb(f"u8T{i}", [128, 8, 128], BF16) for i in range(2)]
        ql = [P.sb(f"ql{i}", [128, 5, 512], F32) for i in range(1)]
        sq = [P.sb(f"sq{i}", [128, 5, 512], BF16) for i in range(1)]
        rq = [P.sb(f"rq{i}", [128, 2, 512], F32) for i in range(1)]
        qlo = [P.sb(f"qlo{i}", [128, 5, 512], BF16) for i in range(1)]
        kp = [P.sb(f"kp{i}", [64, 512], F32) for i in range(1)]
        kg = [P.sb(f"kg{i}", [64, 512], F32) for i in range(1)]
        sqk = [P.sb(f"sqk{i}", [64, 512], BF16) for i in range(1)]
        ssko = [P.sb(f"ssko{i}", [128, 512], F32) for i in range(1)]
        kt1 = [P.sb(f"kt1{i}", [64, 512], F32) for i in range(1)]
        kt2 = [P.sb(f"kt2{i}", [64, 512], F32) for i in range(1)]
        kpo = [P.sb(f"kpo{i}", [64, 512], BF16) for i in range(1)]
        posi = [P.sb(f"posi{i}", [64, 512], I32) for i in range(1)]
        ang = [P.sb(f"ang{i}", [64, 512], F32) for i in range(1)]
        rti = [P.sb(f"rti{i}", [64, 512], I32) for i in range(1)]
        rtf = [P.sb(f"rtf{i}", [64, 512], F32) for i in range(1)]
        cs = [P.sb(f"cs{i}", [64, 2, 512], F32) for i in range(1)]
        gt = [P.sb(f"gt{i}", [128, 512], BF16) for i in range(4)]
        banks = [P.ps(f"b{i}", [128, 512], F32) for i in range(6)]
        tpb = [P.ps(f"tp{i}", [128, 8, 128], BF16) for i in range(2)]
        bank_rr = RR(range(6))
        evac_rr = RR(["vector", "scalar"])
        gt_rr = RR(range(4))
        dq_rr = RR(["sync", "gpsimd"])

        def load_x(blk):
            b2 = blk % 2
            P.dma("sync", xs[b2][:], x[blk * 512:(blk + 1) * 512, :].rearrange("(j p) d -> p j d", p=128), writes=[f"xs{b2}"], key=f"xs{b2}")

        def rope_tables(blk):
            c0, c1 = blk * 512, (blk + 1) * 512
            PS, ANG, RTI, RTF, CS = posi[0], ang[0], rti[0], rtf[0], cs[0]
            rk = "rr0"
            P.dma("sync", PS[:], pos[c0:c1].partition_broadcast(64), writes=["posi0"], key="posi0")
            P.copy("gpsimd", ANG[:], PS[:], ["posi0"], [rk + "_ang"])
            P.ts("gpsimd", ANG[:], ANG[:], invf[:, 0:1], 0.0, ALU.mult, ALU.add, [rk + "_ang", "invf"], [rk + "_ang"])
            range_reduce_sin(P, "gpsimd", ANG[:], CS[:, 1, :], RTI[:], RTF[:], rk, ("cs0", 1), shift=0.0)
            P.act(ANG[:], RTF[:], AF.Abs, [rk + "_tf", rk + "_ang"], [rk + "_ang"])
            P.act(CS[:, 0, :], ANG[:], AF.Sin, [rk + "_ang"], [("cs0", 0)], scale=-1.0, bias=PI / 2)
            P.dma("sync", rope_s[:, :, c0:c1].rearrange("a p t -> p a t"), CS[:], reads=[("cs0", 0), ("cs0", 1)], key="cs0")

        def front(blk):
            sup, half = blk // 2, blk % 2
            b2 = blk % 2
            c0, c1 = blk * 512, (blk + 1) * 512
            b1 = 0
            X, XN, XT = xs[b2], xn[b1], xnT[sup % 2]
            kx, kxn, kxt = f"xs{b2}", f"xn{b1}", (f"xnT{sup % 2}", half)
            for j in range(4):
                col = blk * 4 + j
                P.act(junk[j][:], X[:, j, :], AF.Square, [kx], [f"junk{j}", ("ssx", blk, j)], accum_out=ssx[:, col:col + 1])
            P.act(inv[:, blk * 4:blk * 4 + 4], ssx[:, blk * 4:blk * 4 + 4], AF.Sqrt, [("ssx", blk, j) for j in range(4)], [("inv", blk)],
                  scale=1.0 / D, bias=EPS)
            P.op("vector", "reciprocal", [("inv", blk)], [("inv", blk)], out=inv[:, blk * 4:blk * 4 + 4], in_=inv[:, blk * 4:blk * 4 + 4])
            for j in range(4):
                col = blk * 4 + j
                P.stt(XN[:, j, :], X[:, j, :], inv[:, col:col + 1], gmix[:], ALU.mult, ALU.mult,
                      [kx, ("inv", blk), "gmix"], [(kxn, j)])
                tp, ktp = tpb[j % 2], f"tp{j % 2}"
                for k in range(8):
                    P.tr(tp[:, k, :], XN[:, j, k * 128:(k + 1) * 128], ident[:], [(kxn, j), "ident"], [ktp])
                o0 = half * 512 + j * 128
                P.copy(evac_rr(), XT[:, :, o0:o0 + 128], tp[:, :, :], [ktp], [kxt])

        load_x(0)
        rope_tables(0)
        front(0)
        for blk in range(NB):
            sup, half = blk // 2, blk % 2
            b2 = blk % 2
            c0, c1 = blk * 512, (blk + 1) * 512
            b1 = 0
            if blk + 1 < NB:
                load_x(blk + 1)
            X, XN, XT = xs[b2], xn[b1], xnT[sup % 2]
            kx, kxn, kxt = f"xs{b2}", f"xn{b1}", (f"xnT{sup % 2}", half)

            def proj(col0, m, bank):
                for k in range(8):
                    P.mm(banks[bank][0:m, :], win[:, k, col0:col0 + m], XT[:, k, half * 512:(half + 1) * 512],
                         (k == 0), (k == 7), [kxt, wkey(k, col0)], [("bank", bank)])

            QL, SQ, RQ, QLO = ql[b1], sq[b1], rq[b1], qlo[b1]
            for i in range(5):
                bk = bank_rr()
                proj(512 + i * 128, 128, bk)
                P.copy("scalar", QL[:, i, :], banks[bk][:, :], [("bank", bk)], [(f"ql{b1}", i)])
                P.tt("vector", SQ[:, i, :], banks[bk][:, :], QL[:, i, :], ALU.mult, [("bank", bk), (f"ql{b1}", i)], [(f"sq{b1}", i)])
            for grp, (i0, i1, dim) in enumerate([(0, 3, 384), (3, 5, 256)]):
                bk = bank_rr()
                for i in range(i0, i1):
                    P.mm(banks[bk][:, :], ones_bf[:], SQ[:, i, :], (i == i0), (i == i1 - 1), [(f"sq{b1}", i), "ones"], [("bank", bk)])
                P.act(RQ[:, grp, :], banks[bk][:, :], AF.Sqrt, [("bank", bk)], [(f"rq{b1}", grp)], scale=1.0 / dim, bias=EPS)
                P.op("vector", "reciprocal", [(f"rq{b1}", grp)], [(f"rq{b1}", grp)], out=RQ[:, grp, :], in_=RQ[:, grp, :])
                for i in range(i0, i1):
                    P.stt(QLO[:, i, :], QL[:, i, :], gq[:, i, :], RQ[:, grp, :], ALU.mult, ALU.mult,
                          [(f"ql{b1}", i), (f"rq{b1}", grp), "gq"], [(f"qlo{b1}", i)])
            P.dma("sync", qln_s[:, c0:c1].rearrange("(i p) t -> p i t", p=128), QLO[:, 0:3, :],
                  reads=[(f"qlo{b1}", i) for i in range(3)], key=f"qlo{b1}")
            P.dma("sync", ckvn_s[:, c0:c1].rearrange("(i p) t -> p i t", p=128), QLO[:, 3:5, :],
                  reads=[(f"qlo{b1}", i) for i in range(3, 5)], key=f"qlo{b1}")
            PS, ANG, RTI, RTF, CS = posi[0], ang[0], rti[0], rtf[0], cs[0]
            bk = bank_rr()
            proj(1152, 64, bk)
            KP, KG, SQK = kp[b1], kg[b1], sqk[b1]
            P.copy("scalar", KP[:], banks[bk][0:64, :], [("bank", bk)], [f"kp{b1}"])
            P.tt("vector", SQK[:], KP[:], KP[:], ALU.mult, [f"kp{b1}"], [f"sqk{b1}"])
            P.ts("vector", KG[:], KP[:], gkr[:, 0:1], None, ALU.mult, None, [f"kp{b1}", "gkr"], [f"kg{b1}"])
            bk = bank_rr()
            P.mm(banks[bk][:, :], ones_bf[0:64, :], SQK[:], True, True, [f"sqk{b1}", "ones"], [("bank", bk)])
            P.copy("scalar", ssko[b1][:], banks[bk][:, :], [("bank", bk)], [f"ssko{b1}"])
            P.dma("sync", sskpe_s[:, c0:c1], ssko[b1][:], reads=[f"ssko{b1}"], key=f"ssko{b1}")
            bk = bank_rr()
            P.mm(banks[bk][0:64, :], rm[:], KG[:], True, True, [f"kg{b1}", "rm"], [("bank", bk)])
            P.tt("vector", kt1[b1][:], KG[:], CS[:, 0, :], ALU.mult, [f"kg{b1}", (f"cs{b1}", 0)], [f"kt1{b1}"])
            P.tt("vector", kt2[b1][:], banks[bk][0:64, :], CS[:, 1, :], ALU.mult, [("bank", bk), (f"cs{b1}", 1)], [f"kt2{b1}"])
            P.tt("vector", kpo[b1][:], kt1[b1][:], kt2[b1][:], ALU.add, [f"kt1{b1}", f"kt2{b1}"], [f"kpo{b1}"])
            P.dma("sync", kper_s[:, c0:c1], kpo[b1][:], reads=[f"kpo{b1}"], key=f"kpo{b1}")
            if blk + 1 < NB:
                front(blk + 1)
            for ft in range(16):
                bk = bank_rr()
                proj(1216 + ft * 128, 128, bk)
                gi = gt_rr()
                P.act(gt[gi][:], banks[bk][:, :], AF.Sigmoid, [("bank", bk)], [f"gt{gi}"])
                P.dma("sync" if gi % 2 == 0 else "gpsimd", gates_s[ft * 128:(ft + 1) * 128, c0:c1], gt[gi][:], reads=[f"gt{gi}"], key=f"gt{gi}")
            if half == 1:
                kxt2 = [(f"xnT{sup % 2}", 0), (f"xnT{sup % 2}", 1)]
                for s in range(8):
                    bk = bank_rr()
                    for k in range(8):
                        P.mm(banks[bk][:, :], XT[:, k, s:1024:8], win[:, k, 0:512], (k == 0), (k == 7),
                             kxt2 + [wkey(k, 0)], [("bank", bk)])
                    P.copy(evac_rr(), u8[:, :, s, :], banks[bk][:, :].rearrange("p (g c) -> p g c", c=16), [("bank", bk)], [("u8", s)])
                for gg in range(4):
                    tp, ktp = tpb[gg % 2], f"tp{gg % 2}"
                    for gi in range(8):
                        g = gg * 8 + gi
                        P.tr(tp[:, gi, :], u8[:, g, :, :].rearrange("p s c -> p (s c)"), ident[:],
                             [("u8", s) for s in range(8)] + ["ident"], [ktp])
                    UT = u8T[gg % 2]
                    P.copy(evac_rr(), UT[:], tp[:, :, :], [ktp], [f"u8T{gg % 2}"])
                    P.dma("sync" if gg % 2 == 0 else "gpsimd", u8_s[:, gg * 8:(gg + 1) * 8, sup * 128:(sup + 1) * 128], UT[:], reads=[f"u8T{gg % 2}"],
                          key=f"u8T{gg % 2}")
            if blk + 1 < NB:
                rope_tables(blk + 1)
    if stop_after == 1:
        gstack.close()
        return nc, dbg

    with Phase(nc, "pa", G) as P:
        if pad:
            P.sb("pad", [128, pad], F32)
        qlnT = P.sb("qlnT", [128, 3, L], BF16)
        ckvnT = P.sb("ckvnT", [128, 2, L], BF16)
        kper = P.sb("kper", [64, L], BF16)
        sskpe = P.sb("sskpe", [128, L], F32)
        for blk in range(NB):
            c0, c1 = blk * 512, (blk + 1) * 512
            P.dma("sync", qlnT[:, :, c0:c1], qln_s[:, c0:c1].rearrange("(i p) t -> p i t", p=128), writes=["lat"], key="lat")
            P.dma("sync", ckvnT[:, :, c0:c1], ckvn_s[:, c0:c1].rearrange("(i p) t -> p i t", p=128), writes=["lat"], key="lat")
            P.dma("sync", kper[:, c0:c1], kper_s[:, c0:c1], writes=["lat"], key="lat")
            P.dma("sync", sskpe[:, c0:c1], sskpe_s[:, c0:c1], writes=["lat"], key="lat")
        wqb = P.sb("wqb", [128, 3, 1536], BF16)
        wkvb = P.sb("wkvb", [128, 2, 2048], BF16)
        cast_load(P, wqb, w_q_b, 3, "wqb", "wqb")
        cast_load(P, wkvb, w_kv_b, 2, "wkvb", "wkvb")
        WQB = [("wqb", k) for k in range(3)]
        WKVB = [("wkvb", k) for k in range(2)]
        gqn = P.sb("gqn", [128, 2], F32)
        gkn = P.sb("gkn", [128, 1], F32)
        P.dma("sync", gqn[:, 0:1], bass.AP(q_norm.tensor, 0, [[1, 128], [1, 1]]), writes=["gqn"], key="gqn")
        P.dma("sync", gqn[0:64, 1:2], bass.AP(q_norm.tensor, 128, [[1, 64], [1, 1]]), writes=["gqn"], key="gqn")
        P.dma("sync", gkn[:, 0:1], bass.AP(k_norm.tensor, 0, [[1, 128], [1, 1]]), writes=["gkn"], key="gkn")
        identf = P.sb("identf", [128, 128], F32)
        ident = P.sb("ident", [128, 128], BF16)
        ones_bf = P.sb("ones", [128, 128], BF16)
        make_ident(P, identf, ident)
        P.op("gpsimd", "memset", [], ["ones"], ap=ones_bf[:], constant=1.0)
        rm = P.sb("rm", [64, 64], F32)
        rm2 = P.sb("rm2", [64, 64], F32)
        make_rm(P, rm, rm2)
        NT = L // 128
        QnT = [P.sb(f"QnT{i}", [128, L], BF16) for i in range(2)]
        QrT = [P.sb(f"QrT{i}", [128, L], BF16) for i in range(2)]
        KnT = [P.sb(f"KnT{i}", [128, L], BF16) for i in range(2)]
        KrT = [P.sb(f"KrT{i}", [128, L], BF16) for i in range(2)]
        V = [P.sb(f"V{i}", [128, NT, 130], BF16) for i in range(2)]
        for i in range(2):
            P.op("gpsimd", "memset", [], [("Vones", i)], ap=V[i][:, :, 128:130], constant=1.0)
            P.op("gpsimd", "memset", [], [("Qz", i)], ap=QrT[i][64:128, :], constant=0.0)
            P.op("gpsimd", "memset", [], [("Kz", i)], ap=KrT[i][64:128, :], constant=0.0)
        sqa = P.sb("sqa", [128, 512], BF16)
        sqb = P.sb("sqb", [128, 512], BF16)
        P.op("gpsimd", "memset", [], ["sqb_hi"], ap=sqb[64:128, :], constant=0.0)
        rq = P.sb("rq", [128, 512], F32)
        rq2 = P.sb("rq2", [128, 512], F32)
        qr = P.sb("qr", [64, 512], F32)
        t1 = P.sb("t1", [64, 512], F32)
        t2 = P.sb("t2", [64, 512], F32)
        CS = P.sb("cs", [64, 2, 512], F32)
        pT = [P.sb(f"pT{i}", [128, 512], BF16) for i in range(4)]
        atT = [P.sb(f"atT{i}", [128, 512], BF16) for i in range(2)]
        acc = [P.sb(f"acc{i}", [128, 512], F32) for i in range(2)]
        rden = P.sb("rden", [128, 512], F32)
        ones_f = P.sb("ones_f", [128, 128], F32)
        P.op("gpsimd", "memset", [], ["ones_f"], ap=ones_f[:], constant=1.0)
        pb = [P.ps(f"pb{i}", [128, 512], F32) for i in range(2)]
        oT = [P.ps(f"oT{i}", [128, 512], F32) for i in range(2)]
        dn = P.ps("dn", [128, 512], F32)
        st = [P.ps(f"st{i}", [128, 512], F32) for i in range(3)]
        st_rr, pb_rr, pT_rr, atT_rr, oT_rr = RR(range(3)), RR(range(2)), RR(range(4)), RR(range(2)), RR(range(2))
        ev_rr = RR(["vector", "scalar"])
        SCALE = 192.0 ** -0.5

        def prep(h, blk):
            hb = h % 2
            c0, c1 = blk * 512, (blk + 1) * 512
            P.dma("sync", CS[:], rope_s[:, :, c0:c1].rearrange("a p t -> p a t"), writes=["cs"], key="cs")
            b1 = pb_rr()
            for k in range(3):
                P.mm(pb[b1][:, :], wqb[:, k, h * 192:h * 192 + 128], qlnT[:, k, c0:c1], k == 0, k == 2,
                     [("wqb", k), "lat"], [("pb", b1)])
            P.act(sqa[:], pb[b1][:, :], AF.Square, [("pb", b1)], ["sqa"])
            b2 = pb_rr()
            for k in range(3):
                P.mm(pb[b2][0:64, :], wqb[:, k, h * 192 + 128:h * 192 + 192], qlnT[:, k, c0:c1], k == 0, k == 2,
                     [("wqb", k), "lat"], [("pb", b2)])
            P.act(sqb[0:64, :], pb[b2][0:64, :], AF.Square, [("pb", b2)], ["sqb"])
            P.ts("vector", qr[:], pb[b2][0:64, :], gqn[0:64, 1:2], None, ALU.mult, None, [("pb", b2), "gqn"], ["qr"])
            yield
            b3 = pb_rr()
            P.mm(pb[b2][:, :], ones_bf[:, :], sqa[:], True, False, ["ones", "sqa"], [("pb", b2)])
            P.mm(pb[b2][:, :], ones_bf[:, :], sqb[:], False, True, ["ones", "sqb", "sqb_hi"], [("pb", b2)])
            yield
            P.act(rq[:], pb[b2][:, :], AF.Ln, [("pb", b2)], ["rq"], scale=1.0 / 192, bias=EPS)
            P.act(rq[:], rq[:], AF.Exp, ["rq"], ["rq"], scale=-0.5)
            P.stt(QnT[hb][:, c0:c1], pb[b1][:, :], gqn[:, 0:1], rq[:], ALU.mult, ALU.mult, [("pb", b1), "gqn", "rq"], [("QnT", hb, blk)])
            P.tt("vector", qr[:], qr[:], rq[0:64, :], ALU.mult, ["qr", "rq"], ["qr"])
            yield
            P.mm(pb[b3][0:64, :], rm[:], qr[:], True, True, ["rm", "qr"], [("pb", b3)])
            yield
            P.tt("vector", t1[:], qr[:], CS[:, 0, :], ALU.mult, ["qr", "cs"], ["t1"])
            P.tt("vector", t2[:], pb[b3][0:64, :], CS[:, 1, :], ALU.mult, [("pb", b3), "cs"], ["t2"])
            P.tt("vector", QrT[hb][0:64, c0:c1], t1[:], t2[:], ALU.add, ["t1", "t2"], [("QrT", hb, blk)])
            yield
            b4 = pb_rr()
            for k in range(2):
                P.mm(pb[b4][:, :], wkvb[:, k, h * 256:h * 256 + 128], ckvnT[:, k, c0:c1], k == 0, k == 1,
                     [("wkvb", k), "lat"], [("pb", b4)])
            P.act(sqa[:], pb[b4][:, :], AF.Square, [("pb", b4)], ["sqa"])
            yield
            b5 = pb_rr()
            P.mm(pb[b5][:, :], ones_bf[:, :], sqa[:], True, True, ["ones", "sqa"], [("pb", b5)])
            yield
            P.tt("vector", rq2[:], pb[b5][:, :], sskpe[:, c0:c1], ALU.add, [("pb", b5), "lat"], ["rq2"])
            P.act(rq2[:], rq2[:], AF.Ln, ["rq2"], ["rq2"], scale=1.0 / 192, bias=EPS)
            P.act(rq2[:], rq2[:], AF.Exp, ["rq2"], ["rq2"], scale=-0.5)
            P.stt(KnT[hb][:, c0:c1], pb[b4][:, :], gkn[:, 0:1], rq2[:], ALU.mult, ALU.mult, [("pb", b4), "gkn", "rq2"], [("KnT", hb, blk)])
            P.tt("vector", KrT[hb][0:64, c0:c1], kper[:, c0:c1], rq2[0:64, :], ALU.mult, ["lat", "rq2"], [("KrT", hb, blk)])
            yield
            b6 = pb_rr()
            for j in range(4):
                for k in range(2):
                    P.mm(pb[b6][:, j * 128:(j + 1) * 128], ckvnT[:, k, c0 + j * 128:c0 + (j + 1) * 128],
                         wkvb[:, k, h * 256 + 128:h * 256 + 256], k == 0, k == 1, [("wkvb", k), "lat"], [("pb", b6)])
            P.copy(ev_rr(), V[hb][:, blk * 4:(blk + 1) * 4, 0:128], pb[b6][:, :].rearrange("p (j d) -> p j d", d=128),
                   [("pb", b6)], [("V", hb, blk)])

        def attn(h, qb, gen):
            hb = h % 2
            nkt = 4 * qb + 4
            o_ = oT_rr()

            def qk(kt):
                i = kt - 4 * qb
                cq0 = 128 * i if i > 0 else 0
                kblk = kt // 4
                s_ = st_rr()
                P.mm(st[s_][:, cq0:512], KnT[hb][:, kt * 128:(kt + 1) * 128], QnT[hb][:, qb * 512 + cq0:(qb + 1) * 512], True, False,
                     [("KnT", hb, kblk), ("QnT", hb, qb)], [("st", s_)])
                P.mm(st[s_][:, cq0:512], KrT[hb][:, kt * 128:(kt + 1) * 128], QrT[hb][:, qb * 512 + cq0:(qb + 1) * 512], False, True,
                     [("KrT", hb, kblk), ("QrT", hb, qb), ("Qz", hb), ("Kz", hb)], [("st", s_)])
                pi = pT_rr()
                P.act(pT[pi][:, cq0:512], st[s_][:, cq0:512], AF.Exp, [("st", s_)], [("pT", pi)], scale=SCALE)
                if i >= 0:
                    P.op("gpsimd", "affine_select", [("pT", pi)], [("pT", pi)], out=pT[pi][:, cq0:cq0 + 128], in_=pT[pi][:, cq0:cq0 + 128],
                         pattern=[[1, 128]], compare_op=ALU.is_ge, fill=0.0, base=0, channel_multiplier=-1)
                return pi, cq0

            def pv(kt, pi, cq0):
                kblk = kt // 4
                P.mm(oT[o_][:, cq0:512], V[hb][:, kt, 0:128], pT[pi][:, cq0:512], kt == 0, kt == nkt - 1,
                     [("pT", pi), ("V", hb, kblk)], [("oT", o_)])
                a = 1 if kt % 4 == 3 else 0
                eng = "vector" if a == 0 else "gpsimd"
                if kt == 0:
                    P.copy(eng, acc[0][:, :], pT[pi][:, :], [("pT", pi)], [("acc", 0)])
                else:
                    P.tt(eng, acc[a][:, cq0:512], acc[a][:, cq0:512], pT[pi][:, cq0:512], ALU.add, [("pT", pi), ("acc", a)], [("acc", a)])

            P.op("gpsimd", "memset", [], [("acc", 1)], ap=acc[1][:], constant=0.0)
            pend = [qk(0), qk(1)]
            for kt in range(nkt):
                if kt + 2 < nkt:
                    pend.append(qk(kt + 2))
                pv(kt, *pend.pop(0))
                if gen is not None and kt % 2 == 1:
                    next(gen, None)
            P.tt("vector", acc[0][:, :], acc[0][:, :], acc[1][:, :], ALU.add, [("acc", 0), ("acc", 1)], [("acc", 0)])
            P.mm(dn[:, :], ones_f[:, :], acc[0][:, :], True, True, ["ones_f", ("acc", 0)], [("dn", 0)])
            P.act(rden[:, :], dn[:, :], AF.Ln, [("dn", 0)], ["rden"])
            P.act(rden[:, :], rden[:, :], AF.Exp, ["rden"], ["rden"], scale=-1.0)
            ai = atT_rr()
            P.tt("vector", atT[ai][:, :], oT[o_][:, :], rden[:, :], ALU.mult, [("oT", o_), "rden"], [("atT", ai)])
            P.dma("sync", attnT_s[h * 128:(h + 1) * 128, qb * 512:(qb + 1) * 512], atT[ai][:], reads=[("atT", ai)], key=f"atT{ai}")
            if gen is not None:
                for _ in gen:
                    pass

        for blk in range(NB):
            for _ in prep(0, blk):
                pass
        for h in range(8):
            for qb in range(NB):
                attn(h, qb, prep(h + 1, qb) if h + 1 < 8 else None)
    if stop_after == 2:
        gstack.close()
        return nc, dbg

    def bcast_last(ap, n):
        dims = [list(d) for d in ap.ap]
        assert dims[-1][1] == 1
        dims[-1] = [0, n]
        return bass.AP(ap.tensor, ap.offset, dims)

    if "nossm" not in debug:
        NCT = NC // 128
        sstack = contextlib.ExitStack()
        sb_p = lambda name, shape, dt: sstack.enter_context(nc.sbuf_tensor("ss_" + name, list(shape), dt))
        Wb = [sb_p(f"Wb{i}", [128, 16, 128], BF16) for i in range(2)]
        Toep = sb_p("Toep", [128, 32, 128], BF16)
        Cz = [sb_p(f"Cz{i}", [128, 32, 128], BF16) for i in range(2)]
        ar8 = sb_p("ar8", [128, 16], F32)
        th8 = sb_p("th8", [128, 16], F32)
        with Phase(nc, "s0", G) as P:
            if pad:
                P.sb("pad", [128, pad], F32)
            are = P.sb("are", [128, 16, 1], F32)
            aim = P.sb("aim", [128, 16, 1], F32)
            ldt = P.sb("ldt", [128, 16, 1], F32)
            bre = P.sb("bre", [128, 16, 16], F32)
            bim = P.sb("bim", [128, 16, 16], F32)
            cre = P.sb("cre", [128, 16, 16, 1], F32)
            cim = P.sb("cim", [128, 16, 16, 1], F32)
            dbc = P.sb("dbc", [128, 32, 16], F32)
            for gg in range(2):
                pr = slice(gg * 64, (gg + 1) * 64)
                P.dma("sync", are[pr, :, :], bass.AP(ssm_a_re.tensor, gg * 64, [[1, 64], [128, 16], [1, 1]]), writes=["are"], key="ld_are",
                      allow_slow_non_contiguous=True)
                P.dma("sync", aim[pr, :, :], bass.AP(ssm_a_im.tensor, gg * 64, [[1, 64], [128, 16], [1, 1]]), writes=["aim"], key="ld_aim",
                      allow_slow_non_contiguous=True)
                P.dma("sync", ldt[pr, :, :], bass.AP(ssm_log_dt.tensor, gg, [[0, 64], [2, 16], [1, 1]]), writes=["ldt"], key="ld_ldt",
                      allow_slow_non_contiguous=True)
                P.dma("sync", bre[pr, :, :], bass.AP(ssm_b_re.tensor, gg * 1024, [[16, 64], [2048, 16], [1, 16]]), writes=["bre"], key="ld_bre")
                P.dma("sync", bim[pr, :, :], bass.AP(ssm_b_im.tensor, gg * 1024, [[16, 64], [2048, 16], [1, 16]]), writes=["bim"], key="ld_bim")
            P.dma("sync", dbc[:].rearrange("p g c -> p (g c)"), ssm_d.rearrange("g c -> (g c)").partition_broadcast(128), writes=["dbc"], key="ld_dbc")
            LD = ["are", "aim", "ldt", "bre", "bim", "cre", "cim", "dbc"]
            identf = P.sb("identf", [128, 128], F32)
            ident = P.sb("ident", [128, 128], BF16)
            make_ident(P, identf, ident)
            cnat = [P.sb(f"cnat{i}", [128, 4, 2, 64], F32) for i in range(2)]
            ctp = [P.ps(f"ps{i}", [128, 512], F32) for i in range(4)]
            for i, src in enumerate((ssm_c_re, ssm_c_im)):
                for dup in range(2):
                    P.dma("sync", cnat[i][:, :, dup, :], bass.AP(src.tensor, 0, [[64, 128], [8192, 4], [1, 64]]), writes=[f"cnat{i}"], key=f"ld_cnat{i}")
                for t in range(4):
                    P.tr(ctp[i * 2 + t // 2][:, (t % 2) * 128:(t % 2) * 128 + 128], cnat[i][:, t, :, :].rearrange("p d q -> p (d q)"), identf[:],
                         [f"cnat{i}", "identf"], [("ps", i * 2 + t // 2)])
                dst = (cre, cim)[i]
                for t in range(4):
                    for gg in range(2):
                        pr = slice(gg * 64, (gg + 1) * 64)
                        src_v = ctp[i * 2 + t // 2][pr, (t % 2) * 128:(t % 2) * 128 + 128].rearrange("p (jj g c) -> p jj g c", g=2, c=16)[:, :, gg, :]
                        P.copy("vector" if gg == 0 else "scalar", dst[pr, t * 4:(t + 1) * 4, :, :].rearrange("p j c o -> p j (c o)"), src_v,
                               [("ps", i * 2 + t // 2)], [("cre", "cim")[i]])
            dtt = P.sb("dtt", [128, 16], F32)
            ar = P.sb("ar", [128, 16], F32)
            th = P.sb("th", [128, 16], F32)
            A2 = lambda t: t[:].rearrange("p j o -> p (j o)")
            P.act(dtt[:], A2(ldt), AF.Exp, LD, ["dtt"])
            P.tt("vector", ar[:], A2(are), dtt[:], ALU.mult, LD + ["dtt"], ["ar"])
            P.tt("vector", th[:], A2(aim), dtt[:], ALU.mult, LD + ["dtt"], ["th"])
            P.ts("vector", ar8[:], ar[:], 8.0, None, ALU.mult, None, ["ar"], ["ar8"])
            P.ts("vector", th8[:], th[:], 8.0, None, ALU.mult, None, ["th"], ["th8"])
            kki = P.sb("kki", [128, 16], I32)
            kk = P.sb("kk", [128, 16], F32)
            P.op("gpsimd", "iota", [], ["kki"], out=kki[:], pattern=[[1, 16]], base=-7, channel_multiplier=0)
            P.copy("gpsimd", kk[:], kki[:], ["kki"], ["kk"])
            mag = P.sb("mag", [128, 16, 16], F32)
            ang = P.sb("ang", [128, 16, 16], F32)
            rti = P.sb("rti", [128, 256], I32)
            rtf = P.sb("rtf", [128, 256], F32)
            sn = P.sb("sn", [128, 16, 16], F32)
            cs = P.sb("cs", [128, 16, 16], F32)
            LRE = P.sb("LRE", [128, 16, 16], F32)
            LIM = P.sb("LIM", [128, 16, 16], F32)
            for j in range(16):
                P.act(mag[:, j, :], kk[:], AF.Exp, ["kk", "ar"], [("mag", j)], scale=ar[:, j:j + 1])
                P.ts("gpsimd", ang[:, j, :], kk[:], th[:, j:j + 1], None, ALU.mult, None, ["kk", "th"], ["rr_ang"])
            F2 = lambda t: t[:].rearrange("p j k -> p (j k)")
            range_reduce_sin(P, "gpsimd", F2(ang), F2(cs), rti[:], rtf[:], "rr", "cs", shift=PI / 2)
            range_reduce_sin(P, "gpsimd", F2(ang), F2(sn), rti[:], rtf[:], "rr", "sn", shift=0.0)
            MAG = [("mag", j) for j in range(16)]
            P.tt("vector", F2(LRE), F2(mag), F2(cs), ALU.mult, MAG + ["cs"], ["LRE"])
            P.tt("vector", F2(LIM), F2(mag), F2(sn), ALU.mult, MAG + ["sn"], ["LIM"])
            nr = P.sb("nr", [128, 16], F32)
            ni = P.sb("ni", [128, 16], F32)
            den = P.sb("den", [128, 16], F32)
            tq1 = P.sb("tq1", [128, 16], F32)
            tq2 = P.sb("tq2", [128, 16], F32)
            bere = P.sb("bere", [128, 16], F32)
            beim = P.sb("beim", [128, 16], F32)
            P.ts("vector", nr[:], LRE[:, :, 8], -1.0, None, ALU.add, None, ["LRE"], ["nr"])
            P.copy("vector", ni[:], LIM[:, :, 8], ["LIM"], ["ni"])
            P.tt("vector", den[:], A2(are), A2(are), ALU.mult, LD, ["den"])
            P.tt("vector", tq1[:], A2(aim), A2(aim), ALU.mult, LD, ["tq1"])
            P.tt("vector", den[:], den[:], tq1[:], ALU.add, ["den", "tq1"], ["den"])
            P.op("vector", "reciprocal", ["den"], ["den"], out=den[:], in_=den[:])
            P.tt("vector", tq1[:], nr[:], A2(are), ALU.mult, ["nr"] + LD, ["tq1"])
            P.tt("vector", tq2[:], ni[:], A2(aim), ALU.mult, ["ni"] + LD, ["tq2"])
            P.tt("vector", tq1[:], tq1[:], tq2[:], ALU.add, ["tq1", "tq2"], ["tq1"])
            P.tt("vector", bere[:], tq1[:], den[:], ALU.mult, ["tq1", "den"], ["bere"])
            P.tt("vector", tq1[:], ni[:], A2(are), ALU.mult, ["ni"] + LD, ["tq1"])
            P.tt("vector", tq2[:], nr[:], A2(aim), ALU.mult, ["nr"] + LD, ["tq2"])
            P.tt("vector", tq1[:], tq1[:], tq2[:], ALU.subtract, ["tq1", "tq2"], ["tq1"])
            P.tt("vector", beim[:], tq1[:], den[:], ALU.mult, ["tq1", "den"], ["beim"])
            Bre = P.sb("Bre", [128, 16, 16], F32)
            Bim = P.sb("Bim", [128, 16, 16], F32)
            u1 = P.sb("u1", [128, 16, 16], F32)
            u2 = P.sb("u2", [128, 16, 16], F32)
            bb_re = bcast_last(bere[:].rearrange("p (j o) -> p j o", o=1), 16)
            bb_im = bcast_last(beim[:].rearrange("p (j o) -> p j o", o=1), 16)
            P.tt("vector", u1[:], bre[:], bb_re, ALU.mult, LD + ["bere"], ["u1"])
            P.tt("vector", u2[:], bim[:], bb_im, ALU.mult, LD + ["beim"], ["u2"])
            P.tt("vector", Bre[:], u1[:], u2[:], ALU.subtract, ["u1", "u2"], ["Bre"])
            P.tt("vector", u1[:], bim[:], bb_re, ALU.mult, LD + ["bere"], ["u1"])
            P.tt("vector", u2[:], bre[:], bb_im, ALU.mult, LD + ["beim"], ["u2"])
            P.tt("vector", Bim[:], u1[:], u2[:], ALU.add, ["u1", "u2"], ["Bim"])
            Pw = [P.sb(f"Pw{i}", [128, 16, 8, 16], F32) for i in range(2)]
            Pm = [P.sb(f"Pm{i}", [128, 16, 8, 16], F32) for i in range(2)]
            Qre = P.sb("Qre", [128, 16, 9, 16], F32)
            Qin = P.sb("Qin", [128, 16, 9, 16], F32)
            v1 = [P.sb(f"v1{i}", [128, 16, 16], F32) for i in range(2)]
            v2 = [P.sb(f"v2{i}", [128, 16, 16], F32) for i in range(2)]
            eng_rr = RR(["vector", "gpsimd"])

            def cmul(out_re, out_im, are_, aim_, kidx, rk, wk_re, wk_im, neg_im=False):
                lr = bcast_last(LRE[:, :, kidx:kidx + 1], 16)
                li = bcast_last(LIM[:, :, kidx:kidx + 1], 16)
                e = eng_rr()
                i = 0 if e == "vector" else 1
                P.tt(e, v1[i][:], are_, lr, ALU.mult, rk + ["LRE"], [f"v1{i}"])
                P.tt(e, v2[i][:], aim_, li, ALU.mult, rk + ["LIM"], [f"v2{i}"])
                P.tt(e, out_re, v1[i][:], v2[i][:], ALU.subtract, [f"v1{i}", f"v2{i}"], [wk_re])
                P.tt(e, v1[i][:], are_, li, ALU.mult, rk + ["LIM"], [f"v1{i}"])
                P.tt(e, v2[i][:], aim_, lr, ALU.mult, rk + ["LRE"], [f"v2{i}"])
                if neg_im:
                    P.stt(out_im, v1[i][:], -1.0, v2[i][:], ALU.mult, ALU.subtract, [f"v1{i}", f"v2{i}"], [wk_im]) if e == "vector" else (
                        P.tt(e, v1[i][:], v1[i][:], v2[i][:], ALU.add, [f"v1{i}", f"v2{i}"], [f"v1{i}"]),
                        P.ts(e, out_im, v1[i][:], -1.0, None, ALU.mult, None, [f"v1{i}"], [wk_im]))
                else:
                    P.tt(e, out_im, v1[i][:], v2[i][:], ALU.add, [f"v1{i}", f"v2{i}"], [wk_im])

            for s_ in range(8):
                cmul(Pw[0][:, :, s_, :], Pw[1][:, :, s_, :], Bre[:], Bim[:], 14 - s_, ["Bre", "Bim"], ("Pw0", s_), ("Pw1", s_))
                cmul(Pm[0][:, :, s_, :], Pm[1][:, :, s_, :], Bre[:], Bim[:], 7 - s_, ["Bre", "Bim"], ("Pm0", s_), ("Pm1", s_))
            C3 = lambda t: t[:].rearrange("p j c o -> p j (c o)")
            for s_ in range(9):
                cmul(Qre[:, :, s_, :], Qin[:, :, s_, :], C3(cre), C3(cim), 7 + s_, LD, ("Qre", s_), ("Qin", s_), neg_im=True)
            PW = [[(f"Pw{i}", s_) for s_ in range(8)] for i in range(2)]
            PM = [[(f"Pm{i}", s_) for s_ in range(8)] for i in range(2)]
            QRE = [("Qre", s_) for s_ in range(9)]
            QIN = [("Qin", s_) for s_ in range(9)]
            pbk = ctp
            pb_rr = RR(range(4))
            ev_rr = RR(["vector", "scalar"])
            for i in range(2):
                for jq in range(4):
                    bk = pb_rr()
                    for jj in range(4):
                        j = jq * 4 + jj
                        P.tr(pbk[bk][:, jj * 128:(jj + 1) * 128], Pw[i][:, j, :, :].rearrange("p s c -> p (s c)"), identf[:],
                             PW[i] + ["identf"], [("ps", bk)])
                    P.copy(ev_rr(), Wb[i][:, jq * 4:(jq + 1) * 4, :], pbk[bk][:, :].rearrange("p (j n) -> p j n", n=128), [("ps", bk)], [f"Wb{i}"])
            mask = P.sb("mask", [128, 8, 16], F32)
            P.op("gpsimd", "memset", [], ["mask"], ap=mask[:], constant=1.0)
            P.op("gpsimd", "affine_select", ["mask"], ["mask"], out=mask[:], in_=mask[:], pattern=[[16, 8], [0, 16]],
                 compare_op=ALU.is_ge, fill=0.0, base=15, channel_multiplier=-1)
            tz1 = [P.sb(f"tz1{i}", [128, 128], F32) for i in range(2)]
            tz2 = [P.sb(f"tz2{i}", [128, 128], F32) for i in range(2)]
            MK = mask[:].rearrange("p s c -> p (s c)")
            for z_ in range(2):
                P.op("gpsimd", "memset", [], [f"Cz{z_}"], ap=Cz[z_][:], constant=0.0)
            for g in range(32):
                j, gg = g // 2, g % 2
                pr = slice(gg * 64, (gg + 1) * 64)
                bk = pb_rr()
                P.mm(pbk[bk][:, 0:128], Pm[0][pr, j, :, :].rearrange("p s c -> p (s c)"), Qre[pr, j, 0:8, :].rearrange("p s c -> p (s c)"),
                     True, False, PM[0] + QRE, [("ps", bk)])
                P.mm(pbk[bk][:, 0:128], Pm[1][pr, j, :, :].rearrange("p s c -> p (s c)"), Qin[pr, j, 0:8, :].rearrange("p s c -> p (s c)"),
                     False, True, PM[1] + QIN, [("ps", bk)])
                t = g % 2
                P.tt("vector", tz1[t][:], pbk[bk][:, 0:128], MK, ALU.mult, [("ps", bk), "mask"], [f"tz1{t}"])
                dg = bass.AP(dbc.tensor if hasattr(dbc, "tensor") else dbc[:].tensor, dbc[:, g:g + 1, :].offset,
                             [list(dbc[:, g:g + 1, :].ap[0]), [0, 8], [1, 16]])
                P.tt("gpsimd", tz2[t][:].rearrange("p (s c) -> p s c", c=16), identf[:].rearrange("p (s c) -> p s c", c=16), dg, ALU.mult,
                     ["identf", "dbc"] + LD, [f"tz2{t}"])
                P.tt("gpsimd", Toep[:, g, :], tz1[t][:], tz2[t][:], ALU.add, [f"tz1{t}", f"tz2{t}"], ["Toep"])
                P.copy("gpsimd", Cz[0][pr, g, :], Qre[pr, j, 1:9, :].rearrange("p s c -> p (s c)"), QRE + ["Cz0"], ["Cz0"])
                P.copy("gpsimd", Cz[1][pr, g, :], Qin[pr, j, 1:9, :].rearrange("p s c -> p (s c)"), QIN + ["Cz1"], ["Cz1"])
            if "ssm0" in debug:
                P.dma("sync", dbg_out("Wb0", [128, 16, 128], BF16), Wb[0][:], reads=["Wb0"], key="dbg")
                P.dma("sync", dbg_out("Wb1", [128, 16, 128], BF16), Wb[1][:], reads=["Wb1"], key="dbg")
                P.dma("sync", dbg_out("Toep", [128, 32, 128], BF16), Toep[:], reads=["Toep"], key="dbg")
                P.dma("sync", dbg_out("Cz0", [128, 32, 128], BF16), Cz[0][:], reads=["Cz0"], key="dbg")
                P.dma("sync", dbg_out("Cz1", [128, 32, 128], BF16), Cz[1][:], reads=["Cz1"], key="dbg")
                P.dma("sync", dbg_out("th8", [128, 16], F32), th8[:], reads=["th8"], key="dbg")
                P.dma("sync", dbg_out("ar8", [128, 16], F32), ar8[:], reads=["ar8"], key="dbg")
        if stop_after == 2.5:
            sstack.close()
            gstack.close()
            return nc, dbg

        with Phase(nc, "s1", G) as P:
            if pad:
                P.sb("pad", [128, pad], F32)
            U8 = P.sb("U8", [128, 32, NC], BF16)
            for gq in range(4):
                P.dma("sync", U8[:, gq * 8:(gq + 1) * 8, :], u8_s[:, gq * 8:(gq + 1) * 8, :], writes=[("U8", gq)], key=f"U8{gq}")
            wglu = P.sb("wglu", [128, 4, 512], BF16)
            cast_load(P, wglu, w_glu, 4, "wglu", "wglu")
            bglu = P.sb("bglu", [128, 4, 1], F32)
            P.dma("sync", bglu[:], bass.AP(b_glu.tensor, 0, [[1, 128], [128, 4], [1, 1]]), writes=["bglu"], key="bglu", allow_slow_non_contiguous=True)
            identf = P.sb("identf", [128, 128], F32)
            ident = P.sb("ident", [128, 128], BF16)
            make_ident(P, identf, ident)
            X8 = [P.sb(f"X8{i}", [128, 16, NC + 2], BF16) for i in range(2)]
            for i in range(2):
                P.op("gpsimd", "memset", [], [(f"X8{i}", "z")], ap=X8[i][:, :, 0:1], constant=0.0)
            rampi = P.sb("rampi", [128, NC], I32)
            ramp = P.sb("ramp", [128, NC], F32)
            ones = P.sb("onesf", [128, NC], F32)
            P.op("gpsimd", "iota", [], ["rampi"], out=rampi[:], pattern=[[1, NC]], base=0, channel_multiplier=0)
            P.copy("gpsimd", ramp[:], rampi[:], ["rampi"], ["ramp"])
            P.op("gpsimd", "memset", [], ["onesf"], ap=ones[:], constant=1.0)
            ang = P.sb("ang", [128, NC], F32)
            rti = P.sb("rti", [128, NC], I32)
            rtf = P.sb("rtf", [128, NC], F32)
            csn = [P.sb(f"csn{i}", [128, 2, NC], F32) for i in range(2)]
            rho = [P.sb(f"rho{i}", [128, NC], F32) for i in range(2)]
            Ssb = [P.sb(f"Ssb{i}", [128, 2, NC], F32) for i in range(2)]
            w1 = P.sb("w1", [128, NC], F32)
            w2 = P.sb("w2", [128, NC], F32)
            zri = [P.sb(f"zri{i}", [128, 2, NC], F32) for i in range(2)]
            Zri = [P.sb(f"Zri{i}", [128, 2, NC], F32) for i in range(2)]
            q1, q2 = w1, w2
            l1 = [P.ps(f"ps{i}", [128, 512], F32) for i in range(4)]
            yb = [P.ps(f"yb{i}", [128, 512], F32) for i in range(2)]
            tpb = [P.ps(f"tp{i}", [128, 8, 128], BF16) for i in range(2)]
            U8K = [("U8", gq) for gq in range(4)]
            for j in range(16):
                jb = j % 2
                for i in range(2):
                    bk = jb * 2 + i
                    for gg in range(2):
                        g = 2 * j + gg
                        P.mm(l1[bk][gg * 64:(gg + 1) * 64, 0:NC], Wb[i][:, j, gg * 64:(gg + 1) * 64], U8[:, g, :], True, True,
                             [("U8", g // 8)], [("ps", bk)])
                    P.copy("scalar", Ssb[jb][:, i, :], l1[bk][:, 0:NC], [("ps", bk)], [(f"Ssb{jb}", i)])
                P.ts("gpsimd", ang[:], ramp[:], th8[:, j:j + 1], 0.0, ALU.mult, ALU.add, ["ramp"], ["rr_ang"])
                range_reduce_sin(P, "gpsimd", ang[:], csn[jb][:, 1, :], rti[:], rtf[:], "rr", (f"csn{jb}", 1), shift=0.0)
                P.act(ang[:], rtf[:], AF.Abs, ["rr_tf", "rr_ang"], ["rr_ang"])
                P.act(csn[jb][:, 0, :], ang[:], AF.Sin, ["rr_ang"], [(f"csn{jb}", 0)], scale=-1.0, bias=PI / 2)
                P.act(rho[jb][:], ones[:], AF.Exp, ["onesf"], [f"rho{jb}"], scale=ar8[:, j:j + 1])
                CSK = [(f"csn{jb}", 0), (f"csn{jb}", 1)]
                SK = [(f"Ssb{jb}", 0), (f"Ssb{jb}", 1)]
                cs_, sn_ = csn[jb][:, 0, :], csn[jb][:, 1, :]
                sre, sim_ = Ssb[jb][:, 0, :], Ssb[jb][:, 1, :]
                P.tt("vector", w1[:], sre, cs_, ALU.mult, SK + CSK, ["w1"])
                P.tt("vector", w2[:], sim_, sn_, ALU.mult, SK + CSK, ["w2"])
                P.tt("vector", zri[jb][:, 0, :], w1[:], w2[:], ALU.add, ["w1", "w2"], [(f"zri{jb}", 0)])
                P.tt("vector", w1[:], sim_, cs_, ALU.mult, SK + CSK, ["w1"])
                P.tt("vector", w2[:], sre, sn_, ALU.mult, SK + CSK, ["w2"])
                P.tt("vector", zri[jb][:, 1, :], w1[:], w2[:], ALU.subtract, ["w1", "w2"], [(f"zri{jb}", 1)])
                for i in range(2):
                    P.op("vector", "tensor_tensor_scan", [f"rho{jb}", (f"zri{jb}", i)], [(f"Zri{jb}", i)], out=Zri[jb][:, i, :],
                         data0=rho[jb][:], data1=zri[jb][:, i, :], initial=0.0, op0=ALU.mult, op1=ALU.add)
                ZK = [(f"Zri{jb}", 0), (f"Zri{jb}", 1)]
                zre, zim = Zri[jb][:, 0, :], Zri[jb][:, 1, :]
                P.tt("vector", q1[:], zre, cs_, ALU.mult, ZK + CSK, ["w1"])
                P.tt("vector", q2[:], zim, sn_, ALU.mult, ZK + CSK, ["w2"])
                P.tt("vector", X8[0][:, j, 1:NC + 1], q1[:], q2[:], ALU.subtract, ["w1", "w2"], [("X80", j)])
                P.tt("vector", q1[:], zre, sn_, ALU.mult, ZK + CSK, ["w1"])
                P.tt("vector", q2[:], zim, cs_, ALU.mult, ZK + CSK, ["w2"])
                P.tt("vector", X8[1][:, j, 1:NC + 1], q1[:], q2[:], ALU.add, ["w1", "w2"], [("X81", j)])
            zTb = [P.sb(f"zT{i}", [128, 4, 1024], BF16) for i in range(2)]
            sgl = [P.sb(f"sgl{i}", [128, 512], F32) for i in range(2)]
            glo = [P.sb(f"glo{i}", [128, 512], BF16) for i in range(2)]
            z8 = [P.sb(f"z8{i}", [128, 8, 512], BF16) for i in range(2)]
            ysq = [P.sb(f"ysq{i}", [128, 512], F32) for i in range(2)]
            yw = [P.sb(f"yw{i}", [128, 512], F32) for i in range(2)]
            ysg = [P.sb(f"ysg{i}", [128, 512], F32) for i in range(2)]
            yb_rr = RR(range(2))
            ev_rr = RR(["vector", "scalar"])
            for ct in range(NCT):
                zb = ct % 2
                for gq in range(8):
                    b_ = yb_rr()
                    for gi in range(4):
                        g = gq * 4 + gi
                        j = g // 2
                        P.mm(yb[b_][:, gi * 128:(gi + 1) * 128], U8[:, g, ct * 128:(ct + 1) * 128], Toep[:, g, :], True, False,
                             [("U8", g // 8)], [("yb", b_)])
                        P.mm(yb[b_][:, gi * 128:(gi + 1) * 128], X8[0][:, j, ct * 128:(ct + 1) * 128], Cz[0][:, g, :], False, False,
                             [("X80", j), ("X80", "z")], [("yb", b_)])
                        P.mm(yb[b_][:, gi * 128:(gi + 1) * 128], X8[1][:, j, ct * 128:(ct + 1) * 128], Cz[1][:, g, :], False, True,
                             [("X81", j), ("X81", "z")], [("yb", b_)])
                    t = b_
                    P.act(ysq[t][:], yb[b_][:, :], AF.Square, [("yb", b_)], [f"ysq{t}"])
                    P.ts("vector", yw[t][:], ysq[t][:], 0.044715, 1.0, ALU.mult, ALU.add, [f"ysq{t}"], [f"yw{t}"])
                    P.tt("vector", yw[t][:], yw[t][:], yb[b_][:, :], ALU.mult, [f"yw{t}", ("yb", b_)], [f"yw{t}"])
                    P.act(ysg[t][:], yw[t][:], AF.Sigmoid, [f"yw{t}"], [f"ysg{t}"], scale=1.5957691216057308)
                    dst = z8[zb][:, :, gq * 64:(gq + 1) * 64].rearrange("p s (g c) -> p g s c", c=16)
                    P.tt("vector", dst, ysg[t][:].rearrange("p (g s c) -> p g s c", s=8, c=16),
                         yb[b_][:, :].rearrange("p (g s c) -> p g s c", s=8, c=16), ALU.mult, [f"ysg{t}", ("yb", b_)], [(f"z8{zb}", gq)])
                Z8K = [(f"z8{zb}", gq) for gq in range(8)]
                for s_ in range(8):
                    tp, ktp = tpb[s_ % 2], f"tp{s_ % 2}"
                    for m in range(4):
                        P.tr(tp[:, m, :], z8[zb][:, s_, m * 128:(m + 1) * 128], ident[:], Z8K + ["ident"], [ktp])
                    P.copy(ev_rr(), zTb[zb][:, :, s_:1024:8], tp[:, 0:4, :], [ktp], [("zT", zb)])
                for bh in range(2):
                    blk = ct * 2 + bh
                    c0, c1 = blk * 512, (blk + 1) * 512
                    for m in range(4):
                        b_ = yb_rr()
                        for k in range(4):
                            P.mm(yb[b_][:, :], wglu[:, k, m * 128:(m + 1) * 128], zTb[zb][:, k, bh * 512:(bh + 1) * 512], k == 0, k == 3,
                                 [("wglu", k), ("zT", zb)], [("yb", b_)])
                        t = m % 2
                        P.act(sgl[t][:], yb[b_][:, :], AF.Sigmoid, [("yb", b_), "bglu"], [f"sgl{t}"], bias=bglu[:, m, :])
                        P.tt("vector", glo[t][:], zTb[zb][:, m, bh * 512:(bh + 1) * 512], sgl[t][:], ALU.mult, [("zT", zb), f"sgl{t}"], [f"glo{t}"])
                        P.dma("sync", gluT_s[m * 128:(m + 1) * 128, c0:c1], glo[t][:], reads=[f"glo{t}"], key=f"glo{t}")
        sstack.close()

    if "nossm" in debug:
        with Phase(nc, "pz", G) as P:
            zt = P.sb("zt", [128, 512], BF16)
            P.op("gpsimd", "memset", [], ["zt"], ap=zt[:], constant=0.0)
            for blk in range(NB):
                for k in range(4):
                    P.dma("sync", gluT_s[k * 128:(k + 1) * 128, blk * 512:(blk + 1) * 512], zt[:], reads=["zt"], key="z")

    with Phase(nc, "pm", G) as P:
        if pad:
            P.sb("pad", [128, pad], F32)
        wos = P.sb("wos", [128, 4, 1024], BF16)
        wom = P.sb("wom", [128, 8, 1024], BF16)
        wout = P.sb("wout", [128, 8, 1024], BF16)
        cast_load(P, wos, w_o_ssm, 4, "wos", "wos")
        cast_load(P, wom, w_o_mla, 8, "wom", "wom")
        cast_load(P, wout, w_out, 8, "wout", "wout")
        gl = [P.sb(f"gl{i}", [128, 4, 512], BF16) for i in range(2)]
        at = [P.sb(f"at{i}", [128, 8, 512], BF16) for i in range(2)]
        gts = [P.sb(f"gts{i}", [128, 16, 512], BF16) for i in range(2)]
        xs = [P.sb(f"xs{i}", [128, 4, D], F32) for i in range(2)]
        hb = [P.sb(f"hb{i}", [128, 4, D], F32) for i in range(2)]
        mixT = P.sb("mixT", [128, 8, 512], BF16)
        tm1 = [P.sb(f"tm1{i}", [128, 512], F32) for i in range(2)]
        tm2 = [P.sb(f"tm2{i}", [128, 512], F32) for i in range(2)]
        banks = [P.ps(f"b{i}", [128, 512], F32) for i in range(6)]
        bank_rr = RR(range(6))
        for blk in range(NB):
            b2 = blk % 2
            c0, c1 = blk * 512, (blk + 1) * 512
            P.dma("sync", gl[b2][:], gluT_s[:, c0:c1].rearrange("(k p) t -> p k t", p=128), writes=[f"gl{b2}"], key=f"gl{b2}")
            P.dma("sync", at[b2][:], attnT_s[:, c0:c1].rearrange("(k p) t -> p k t", p=128), writes=[f"at{b2}"], key=f"at{b2}")
            P.dma("sync", gts[b2][:], gates_s[:, c0:c1].rearrange("(k p) t -> p k t", p=128), writes=[f"gts{b2}"], key=f"gts{b2}")
            P.dma("sync", xs[b2][:], x[c0:c1, :].rearrange("(j p) d -> p j d", p=128), writes=[f"xs{b2}"], key=f"xs{b2}")
            for m in range(8):
                ba, bb = bank_rr(), bank_rr()
                for k in range(4):
                    P.mm(banks[ba][:, :], wos[:, k, m * 128:(m + 1) * 128], gl[b2][:, k, :], k == 0, k == 3,
                         [("wos", k), f"gl{b2}"], [("bank", ba)])
                for k in range(8):
                    P.mm(banks[bb][:, :], wom[:, k, m * 128:(m + 1) * 128], at[b2][:, k, :], k == 0, k == 7,
                         [("wom", k), f"at{b2}"], [("bank", bb)])
                t = m % 2
                P.tt("vector", tm1[t][:], banks[ba][:, :], gts[b2][:, m, :], ALU.mult, [("bank", ba), f"gts{b2}"], [f"tm1{t}"])
                P.tt("vector", tm2[t][:], banks[bb][:, :], gts[b2][:, 8 + m, :], ALU.mult, [("bank", bb), f"gts{b2}"], [f"tm2{t}"])
                P.tt("gpsimd", mixT[:, m, :], tm1[t][:], tm2[t][:], ALU.add, [f"tm1{t}", f"tm2{t}"], [("mixT", m)])
            for j in range(4):
                for hh in range(2):
                    bc = bank_rr()
                    for k in range(8):
                        P.mm(banks[bc][:, :], mixT[:, k, j * 128:(j + 1) * 128], wout[:, k, hh * 512:(hh + 1) * 512], k == 0, k == 7,
                             [("mixT", k), ("wout", k)], [("bank", bc)])
                    P.tt("vector", hb[b2][:, j, hh * 512:(hh + 1) * 512], banks[bc][:, :], xs[b2][:, j, hh * 512:(hh + 1) * 512], ALU.add,
                         [("bank", bc), f"xs{b2}"], [(f"hb{b2}", j, hh)])
            P.dma("sync", h_s[c0:c1, :].rearrange("(j p) d -> p j d", p=128), hb[b2][:],
                  reads=[(f"hb{b2}", j, hh) for j in range(4) for hh in range(2)], key=f"hb{b2}")
    if stop_after == 3:
        gstack.close()
        return nc, dbg

    with Phase(nc, "pf", G) as P:
        if pad:
            P.sb("pad", [128, pad], F32)
        wup = P.sb("wup", [128, 8, 4096], BF16)
        wdn = P.sb("wdn", [128, 32, 1024], BF16)
        seq = 0
        for c in range(4):
            for k in range(8):
                P.dma("gpsimd", wup[:, k, c * 1024:(c + 1) * 1024], w_up[k * 128:(k + 1) * 128, c * 1024:(c + 1) * 1024],
                      reads=[("cseq", seq - 2)] if seq >= 2 else [], writes=[("wup", k, c), ("cseq", seq)], key=f"wup{c}", max_dma_last_dim=4096)
                seq += 1
            for r in range(c * 8, (c + 1) * 8):
                P.dma("gpsimd", wdn[:, r, :], w_down[r * 128:(r + 1) * 128, :],
                      reads=[("cseq", seq - 2)] if seq >= 2 else [], writes=[("wdn", r), ("cseq", seq)], key=f"wdn{c}", max_dma_last_dim=4096)
                seq += 1
        gmlp = P.sb("gmlp", [128, D], F32)
        P.dma("sync", gmlp[:], norm_mlp.partition_broadcast(128), writes=["gmlp"], key="gmlp")
        identf = P.sb("identf", [128, 128], F32)
        ident = P.sb("ident", [128, 128], BF16)
        make_ident(P, identf, ident)
        NB2 = L // 256
        hs = [P.sb(f"hs{i}", [128, 2, D], F32) for i in range(2)]
        ob_ = [P.sb(f"o{i}", [128, 2, D], F32) for i in range(2)]
        hn = [P.sb(f"hn{i}", [128, 2, D], BF16) for i in range(2)]
        hnTb = [P.sb(f"hnT{i}", [128, 8, 256], BF16) for i in range(2)]
        junk = [P.sb(f"junk{i}", [128, D], BF16) for i in range(2)]
        ss = P.sb("ss", [128, 2 * NB2], F32)
        inv = P.sb("inv", [128, 2 * NB2], F32)
        rl = [P.sb(f"rl{i}", [128, 256], F32) for i in range(2)]
        aT = [P.sb(f"aT{i}", [128, 256], BF16) for i in range(3)]
        acc = [P.ps(f"acc{i}", [128, 512], F32) for i in range(4)]
        ub = [P.ps(f"ub{i}", [128, 512], F32) for i in range(2)]
        tpb = [P.ps(f"tp{i}", [128, 8, 128], BF16) for i in range(2)]
        ub_rr, aT_rr, rl_rr = RR(range(2)), RR(range(3)), RR(range(2))
        ev_rr = RR(["vector", "scalar"])
        def ffront(blk):
            b2 = blk % 2
            r0, r1 = blk * 256, (blk + 1) * 256
            P.dma("sync", hs[b2][:], h_s[r0:r1, :].rearrange("(j p) d -> p j d", p=128), writes=[f"hs{b2}"], key=f"hs{b2}")
            for j in range(2):
                col = blk * 2 + j
                P.act(junk[j][:], hs[b2][:, j, :], AF.Square, [f"hs{b2}"], [f"junk{j}", ("ss", blk, j)], accum_out=ss[:, col:col + 1])
            P.act(inv[:, blk * 2:blk * 2 + 2], ss[:, blk * 2:blk * 2 + 2], AF.Sqrt, [("ss", blk, 0), ("ss", blk, 1)], [("inv", blk)],
                  scale=1.0 / D, bias=EPS)
            P.op("vector", "reciprocal", [("inv", blk)], [("inv", blk)], out=inv[:, blk * 2:blk * 2 + 2], in_=inv[:, blk * 2:blk * 2 + 2])
            for j in range(2):
                col = blk * 2 + j
                P.stt(hn[b2][:, j, :], hs[b2][:, j, :], inv[:, col:col + 1], gmlp[:], ALU.mult, ALU.mult,
                      [f"hs{b2}", ("inv", blk), "gmlp"], [(f"hn{b2}", j)])
                tp, ktp = tpb[j], f"tp{j}"
                for k in range(8):
                    P.tr(tp[:, k, :], hn[b2][:, j, k * 128:(k + 1) * 128], ident[:], [(f"hn{b2}", j), "ident"], [ktp])
                P.copy(ev_rr(), hnTb[b2][:, :, j * 128:(j + 1) * 128], tp[:, :, :], [ktp], [f"hnT{b2}"])

        ffront(0)
        for blk in range(NB2):
            b2 = blk % 2
            r0, r1 = blk * 256, (blk + 1) * 256
            hnT = hnTb[b2]

            def up(f):
                u = ub_rr()
                for k in range(8):
                    P.mm(ub[u][:, 0:256], wup[:, k, f * 128:(f + 1) * 128], hnT[:, k, :], k == 0, k == 7, [("wup", k, f // 8), f"hnT{b2}"], [("ub", u)])
                ri, ai = rl_rr(), aT_rr()
                P.act(rl[ri][:], ub[u][:, 0:256], AF.Relu, [("ub", u)], [f"rl{ri}"])
                P.tt("vector", aT[ai][:], rl[ri][:], rl[ri][:], ALU.mult, [f"rl{ri}"], [f"aT{ai}"])
                return ai

            nxt = up(0)
            for f in range(32):
                ai = nxt
                if f + 1 < 32:
                    nxt = up(f + 1)
                if f == 12 and blk + 1 < NB2:
                    ffront(blk + 1)
                for j in range(2):
                    for hh in range(2):
                        P.mm(acc[j * 2 + hh][:, :], aT[ai][:, j * 128:(j + 1) * 128], wdn[:, f, hh * 512:(hh + 1) * 512], f == 0, f == 31,
                             [f"aT{ai}", ("wdn", f)], [("facc", j * 2 + hh)])
            for j in range(2):
                for hh in range(2):
                    P.tt("vector", ob_[b2][:, j, hh * 512:(hh + 1) * 512], acc[j * 2 + hh][:, :], hs[b2][:, j, hh * 512:(hh + 1) * 512], ALU.add,
                         [("facc", j * 2 + hh), f"hs{b2}"], [(f"o{b2}", j, hh)])
            P.dma("sync", out[r0:r1, :].rearrange("(j p) d -> p j d", p=128), ob_[b2][:],
                  reads=[(f"o{b2}", j, hh) for j in range(2) for hh in range(2)], key=f"o{b2}")
    gstack.close()
    return nc, dbg


def make_in_maps(inputs, L=4096, cores=8):
    half = 32
    inv_freq = (10000.0 ** (-np.arange(half, dtype=np.float32) / half)).astype(np.float32)
    inv_freq2 = np.concatenate([inv_freq, inv_freq]).reshape(64, 1).astype(np.float32)
    maps = []
    for b in range(cores):
        m = {"x": np.ascontiguousarray(inputs["x"][b, :L]), "positions": np.ascontiguousarray(inputs["positions"][b, :L]),
             "inv_freq2": inv_freq2}
        for k, v in inputs.items():
            if k in ("x", "positions"):
                continue
            m[k] = np.ascontiguousarray(np.asarray(v)[0])
        maps.append(m)
    return maps


_NC_CACHE = {}


def kernel(**inputs):
    L = 4096
    if "nc" not in _NC_CACHE:
        _NC_CACHE["nc"] = build(L)[0]
    nc = _NC_CACHE["nc"]
    maps = make_in_maps(inputs, L, 8)
    res = run_bass_kernel_spmd(nc, maps, core_ids=list(range(8)))
    return np.stack([np.asarray(r["out"]) for r in res.results], axis=0).astype(np.float32)
```

```python
import bisect
import contextlib
import math
import numpy as np
import concourse.bass as bass
import concourse.mybir as mybir
from concourse.bass_utils import run_bass_kernel_spmd

F32 = mybir.dt.float32
BF16 = mybir.dt.bfloat16
I32 = mybir.dt.int32
AF = mybir.ActivationFunctionType
ALU = mybir.AluOpType

EPS = 1e-6
D = 1024
DIN = 3264
PI = math.pi
ENGINES = ("tensor", "vector", "scalar", "gpsimd", "sync")


class Sems:
    def __init__(self, nc, stack, n_dma=56):
        self.esem = {e: stack.enter_context(nc.semaphore(f"s_{e}")) for e in ENGINES}
        self.bar = stack.enter_context(nc.semaphore("s_bar"))
        self.pool = {q: [stack.enter_context(nc.semaphore(f"s_{q}{i}")) for i in range(n)]
                     for q, n in (("hw", n_dma), ("sw", 24))}
        self.eng_cnt = {e: 0 for e in ENGINES}
        self.pool_cnt = {q: [0] * len(v) for q, v in self.pool.items()}
        self.phase_no = 0


class Phase:
    def __init__(self, nc, name, G):
        self.nc = nc
        self.G = G
        self.name = name
        self.ops = []
        self.stack = contextlib.ExitStack()

    PSUM_KEYS = ("bank", "pb", "st", "ob", "obT", "tp", "ps", "ub", "yb", "oT", "dn", "facc")

    def is_psum(self, key):
        k0 = key[0] if isinstance(key, tuple) else key
        return isinstance(k0, str) and (k0 in self.PSUM_KEYS or k0.startswith("tp"))

    def sb(self, name, shape, dtype):
        return self.stack.enter_context(self.nc.sbuf_tensor(f"{self.name}_{name}", list(shape), dtype))

    def ps(self, name, shape, dtype=F32):
        return self.stack.enter_context(self.nc.psum_tensor(f"{self.name}_{name}", list(shape), dtype))

    def __enter__(self):
        self.stack.__enter__()
        return self

    def op(self, eng, name, reads=(), writes=(), **kw):
        self.ops.append(dict(eng=eng, fn=(lambda e, name=name, kw=kw: getattr(e, name)(**kw)),
                             reads=tuple(reads), writes=tuple(writes), dma=None))

    def mm(self, out, lhsT, rhs, start, stop, reads, writes):
        self.op("tensor", "matmul", reads, writes, out=out, lhsT=lhsT, rhs=rhs, start=start, stop=stop)

    def tr(self, out, in_, identity, reads, writes):
        self.op("tensor", "transpose", reads, writes, out=out, in_=in_, identity=identity)

    def act(self, out, in_, func, reads, writes, **kw):
        self.op("scalar", "activation", reads, writes, out=out, in_=in_, func=func, **kw)

    def copy(self, eng, out, in_, reads, writes):
        if eng == "scalar":
            self.op("scalar", "copy", reads, writes, out=out, in_=in_)
        else:
            self.op(eng, "tensor_copy", reads, writes, out=out, in_=in_)

    def tt(self, eng, out, in0, in1, op, reads, writes):
        self.op(eng, "tensor_tensor", reads, writes, out=out, in0=in0, in1=in1, op=op)

    def ts(self, eng, out, in0, s1, s2, op0, op1, reads, writes):
        if s2 is None:
            self.op(eng, "tensor_scalar", reads, writes, out=out, in0=in0, scalar1=s1, scalar2=None, op0=op0)
        else:
            self.op(eng, "tensor_scalar", reads, writes, out=out, in0=in0, scalar1=s1, scalar2=s2, op0=op0, op1=op1)

    def stt(self, out, in0, scalar, in1, op0, op1, reads, writes):
        self.op("vector", "scalar_tensor_tensor", reads, writes, out=out, in0=in0, scalar=scalar, in1=in1, op0=op0, op1=op1)

    def dma(self, eng, out, in_, reads=(), writes=(), key=None, **kw):
        assert key is not None
        self.ops.append(dict(eng=eng, fn=(lambda e, out=out, in_=in_, kw=kw: e.dma_start(out=out, in_=in_, **kw)),
                             reads=tuple(reads), writes=tuple(writes), dma=key))

    def __exit__(self, et, ev, tb):
        if et is None:
            self._emit()
        return self.stack.__exit__(et, ev, tb)

    def _emit(self):
        nc = self.nc
        ops = self.ops
        n = len(ops)
        last_w, readers = {}, {}
        deps = [None] * n
        for i, o in enumerate(ops):
            d = set()
            raw = set()
            for r in o["reads"]:
                if r in last_w:
                    d.add(last_w[r])
                    raw.add(last_w[r])
            weff = list(o["writes"]) + [r for r in o["reads"] if self.is_psum(r) and r not in o["writes"]]
            for w in weff:
                if w in last_w:
                    d.add(last_w[w])
                d.update(readers.get(w, ()))
            d.discard(i)
            d = {j for j in d if ops[j]["dma"] is not None or ops[j]["eng"] != o["eng"]
                 or o["eng"] != "tensor" or o["dma"] is not None}
            deps[i] = d
            for w in weff:
                last_w[w] = i
                readers[w] = []
            for r in o["reads"]:
                if r not in weff:
                    readers.setdefault(r, []).append(i)
        need_sig = [False] * n
        for i in range(n):
            for j in deps[i]:
                if ops[j]["dma"] is None:
                    need_sig[j] = True
        G = self.G
        eng_cnt = dict(G.eng_cnt)
        sig_val = [0] * n
        dma_idx = {}
        for i, o in enumerate(ops):
            if o["dma"] is not None:
                dma_idx.setdefault(o["dma"], []).append(i)
            elif need_sig[i]:
                eng_cnt[o["eng"]] += 1
                sig_val[i] = eng_cnt[o["eng"]]
        keys = list(dma_idx)
        kq = {}
        for k in keys:
            qs = {"sw" if ops[i]["eng"] == "gpsimd" else "hw" for i in dma_idx[k]}
            assert len(qs) == 1, (k, qs)
            kq[k] = qs.pop()
        kslot, nq = {}, {"hw": 0, "sw": 0}
        for k in keys:
            kslot[k] = nq[kq[k]]
            nq[kq[k]] += 1
            assert nq[kq[k]] <= len(G.pool[kq[k]]), (self.name, kq[k], nq)
        ksem = {k: G.pool[kq[k]][kslot[k]] for k in keys}
        kbase = {k: G.pool_cnt[kq[k]][kslot[k]] for k in keys}
        G.phase_no += 1
        phase_no = G.phase_no
        per_eng = {e: [] for e in ENGINES}
        for i, o in enumerate(ops):
            per_eng[o["eng"]].append(i)
        with nc.Block() as block:

            def make(e_name):
                idxs = per_eng[e_name]

                def body(e):
                    waited = {}
                    for i in idxs:
                        o = ops[i]
                        want = {}
                        for j in deps[i]:
                            pj = ops[j]
                            if pj["dma"] is not None:
                                k = pj["dma"]
                                sk = ("d", k)
                                v = kbase[k] + 16 * bisect.bisect_left(dma_idx[k], i)
                            else:
                                sk = ("e", pj["eng"])
                                v = sig_val[j]
                            want[sk] = max(want.get(sk, 0), v)
                        for sk, v in sorted(want.items(), key=lambda kv: str(kv[0])):
                            if waited.get(sk, 0) >= v:
                                continue
                            waited[sk] = v
                            e.wait_ge(ksem[sk[1]] if sk[0] == "d" else G.esem[sk[1]], v)
                        ins = o["fn"](e)
                        if o["dma"] is not None:
                            ins.then_inc(ksem[o["dma"]], 16)
                        elif need_sig[i]:
                            ins.then_inc(G.esem[e_name], 1)
                    for k in sorted({ops[i]["dma"] for i in idxs if ops[i]["dma"] is not None}, key=str):
                        e.wait_ge(ksem[k], kbase[k] + 16 * len(dma_idx[k]))
                    e.sem_inc(G.bar, 1)
                    e.wait_ge(G.bar, len(ENGINES) * phase_no)
                return body

            for e_name in ENGINES:
                getattr(block, e_name)(make(e_name))
        G.eng_cnt = eng_cnt
        for k in keys:
            G.pool_cnt[kq[k]][kslot[k]] = kbase[k] + 16 * len(dma_idx[k])


class RR:
    def __init__(self, items):
        self.items = list(items)
        self.i = 0

    def __call__(self):
        v = self.items[self.i % len(self.items)]
        self.i += 1
        return v


def range_reduce_sin(P, eng, ang, out, tmp_i, tmp_f, key, out_key, shift=0.0):
    ka, ki, kf = key + "_ang", key + "_ti", key + "_tf"
    P.ts(eng, tmp_i, ang, 1.0 / (2 * PI), shift / (2 * PI), ALU.mult, ALU.add, [ka], [ki])
    P.copy(eng, tmp_f, tmp_i, [ki], [kf])
    P.ts(eng, tmp_f, tmp_f, -2 * PI, shift, ALU.mult, ALU.add, [kf], [kf])
    P.tt(eng, tmp_f, tmp_f, ang, ALU.add, [kf, ka], [kf])
    P.ts(eng, tmp_f, tmp_f, PI, -PI, ALU.min, ALU.max, [kf], [kf])
    P.act(out, tmp_f, AF.Sin, [kf], [out_key])


VAR = 0


def build(L=4096, debug=(), stop_after=None, pad=0, amode=2, pstage=9, pcut=99):
    assert L % 1024 == 0
    NB = L // 512
    NS = L // 1024
    NC = L // 8
    nc = bass.Bass("TRN2", target_bir_lowering=False)
    dbg = {}

    def din(name, shape, dt=F32):
        return nc.dram_tensor(name, list(shape), dt, kind="ExternalInput").ap()

    x = din("x", [L, D])
    pos = din("positions", [L], I32)
    norm_mix = din("norm_mix", [D])
    w_in = din("w_in", [D, DIN])
    q_a_norm = din("q_a_norm", [384])
    kv_a_norm = din("kv_a_norm", [256])
    w_q_b = din("w_q_b", [384, 1536])
    w_kv_b = din("w_kv_b", [256, 2048])
    q_norm = din("q_norm", [192])
    k_norm = din("k_norm", [192])
    w_o_mla = din("w_o_mla", [1024, 1024])
    ssm_a_re = din("ssm_a_re", [32, 64])
    ssm_a_im = din("ssm_a_im", [32, 64])
    ssm_log_dt = din("ssm_log_dt", [32])
    ssm_b_re = din("ssm_b_re", [32, 64, 16])
    ssm_b_im = din("ssm_b_im", [32, 64, 16])
    ssm_c_re = din("ssm_c_re", [32, 16, 64])
    ssm_c_im = din("ssm_c_im", [32, 16, 64])
    ssm_d = din("ssm_d", [32, 16])
    w_glu = din("w_glu", [512, 512])
    b_glu = din("b_glu", [512])
    w_o_ssm = din("w_o_ssm", [512, 1024])
    w_out = din("w_out", [1024, 1024])
    norm_mlp = din("norm_mlp", [1024])
    w_up = din("w_up", [1024, 4096])
    w_down = din("w_down", [4096, 1024])
    inv_freq2 = din("inv_freq2", [64, 1])
    out = nc.dram_tensor("out", [L, D], F32, kind="ExternalOutput").ap()

    def dbg_out(name, shape, dt=F32):
        t = nc.dram_tensor("dbg_" + name, list(shape), dt, kind="ExternalOutput").ap()
        dbg[name] = t
        return t

    def scratch(name, shape, dt):
        if name in debug:
            return dbg_out(name, shape, dt)
        return nc.dram_tensor("s_" + name, list(shape), dt).ap()

    gates_s = scratch("gates", [2048, L], BF16)
    qln_s = scratch("qln", [384, L], BF16)
    ckvn_s = scratch("ckvn", [256, L], BF16)
    kper_s = scratch("kper", [64, L], BF16)
    sskpe_s = scratch("sskpe", [128, L], F32)
    rope_s = scratch("rope", [2, 64, L], F32)
    attnT_s = scratch("attnT", [1024, L], BF16)
    gluT_s = scratch("gluT", [512, L], BF16)
    h_s = scratch("h", [L, D], F32)
    u8_s = scratch("u8", [128, 32, NC], BF16)

    def cast_load(P, dst_tile, src, rows, key, wkey):
        per = (rows + 3) // 4
        for k in range(rows):
            P.dma("gpsimd", dst_tile[:, k, :], src[k * 128:(k + 1) * 128, :], reads=[("cseq", wkey, k - 2)] if k >= 2 else [],
                  writes=[(wkey, k), ("cseq", wkey, k)], key=f"{key}{k // per}", max_dma_last_dim=4096)

    def make_ident(P, identf, ident):
        P.op("gpsimd", "memset", [], ["identf"], ap=identf[:], constant=1.0)
        P.op("gpsimd", "affine_select", ["identf"], ["identf"], out=identf[:], in_=identf[:], pattern=[[-1, 128]],
             compare_op=ALU.is_equal, fill=0.0, base=0, channel_multiplier=1)
        P.copy("gpsimd", ident[:], identf[:], ["identf"], ["ident"])

    def make_rm(P, rm, rm2):
        P.op("gpsimd", "memset", [], ["rm"], ap=rm[:], constant=-1.0)
        P.op("gpsimd", "affine_select", ["rm"], ["rm"], out=rm[:], in_=rm[:], pattern=[[-1, 64]], compare_op=ALU.is_equal,
             fill=0.0, base=-32, channel_multiplier=1)
        P.op("gpsimd", "memset", [], ["rm2"], ap=rm2[:], constant=1.0)
        P.op("gpsimd", "affine_select", ["rm2"], ["rm2"], out=rm2[:], in_=rm2[:], pattern=[[-1, 64]], compare_op=ALU.is_equal,
             fill=0.0, base=32, channel_multiplier=1)
        P.tt("gpsimd", rm[:], rm[:], rm2[:], ALU.add, ["rm", "rm2"], ["rm"])

    gstack = contextlib.ExitStack()
    G = Sems(nc, gstack)

    with Phase(nc, "p1", G) as P:
        if pad:
            P.sb("pad", [128, pad], F32)
        win = P.sb("win", [128, 8, DIN], BF16)
        cast_load(P, win, w_in, 8, "win", "win")
        gmix = P.sb("gmix", [128, D], F32)
        P.dma("sync", gmix[:], norm_mix.partition_broadcast(128), writes=["gmix"], key="gmix")
        gq = P.sb("gq", [128, 5, 1], F32)
        P.dma("sync", gq[:, 0:3, :], bass.AP(q_a_norm.tensor, 0, [[1, 128], [128, 3], [1, 1]]), writes=["gq"], key="gq", allow_slow_non_contiguous=True)
        P.dma("sync", gq[:, 3:5, :], bass.AP(kv_a_norm.tensor, 0, [[1, 128], [128, 2], [1, 1]]), writes=["gq"], key="gq", allow_slow_non_contiguous=True)
        gkr = P.sb("gkr", [64, 1], F32)
        P.dma("sync", gkr[:], bass.AP(k_norm.tensor, 128, [[1, 64], [1, 1]]), writes=["gkr"], key="gkr")
        invf = P.sb("invf", [64, 1], F32)
        P.dma("sync", invf[:], inv_freq2, writes=["invf"], key="invf")
        identf = P.sb("identf", [128, 128], F32)
        ident = P.sb("ident", [128, 128], BF16)
        ones_bf = P.sb("ones", [128, 128], BF16)
        make_ident(P, identf, ident)
        P.op("gpsimd", "memset", [], ["ones"], ap=ones_bf[:], constant=1.0)
        rm = P.sb("rm", [64, 64], F32)
        rm2 = P.sb("rm2", [64, 64], F32)
        make_rm(P, rm, rm2)

        xs = [P.sb(f"xs{i}", [128, 4, D], F32) for i in range(2)]
        xn = [P.sb(f"xn{i}", [128, 4, D], BF16) for i in range(1)]
        xnT = [P.sb(f"xnT{i}", [128, 8, 1024], BF16) for i in range(2)]
        junk = [P.sb(f"junk{i}", [128, D], BF16) for i in range(4)]
        ssx = P.sb("ssx", [128, 4 * NB], F32)
        inv = P.sb("inv", [128, 4 * NB], F32)
        u8 = P.sb("u8", [128, 32, 8, 16], BF16)
        u8T = [P.sb(f"u8T{i}", [128, 8, 128], BF16) for i in range(2)]
        ql = [P.sb(f"ql{i}", [128, 5, 512], F32) for i in range(1)]
        sq = [P.sb(f"sq{i}", [128, 5, 512], BF16) for i in range(1)]
        rq = [P.sb(f"rq{i}", [128, 2, 512], F32) for i in range(1)]
        qlo = [P.sb(f"qlo{i}", [128, 5, 512], BF16) for i in range(1)]
        kp = [P.sb(f"kp{i}", [64, 512], F32) for i in range(1)]
        kg = [P.sb(f"kg{i}", [64, 512], F32) for i in range(1)]
        sqk = [P.sb(f"sqk{i}", [64, 512], BF16) for i in range(1)]
        ssko = [P.sb(f"ssko{i}", [128, 512], F32) for i in range(1)]
        kt1 = [P.sb(f"kt1{i}", [64, 512], F32) for i in range(1)]
        kt2 = [P.sb(f"kt2{i}", [64, 512], F32) for i in range(1)]
        kpo = [P.sb(f"kpo{i}", [64, 512], BF16) for i in range(1)]
        posi = [P.sb(f"posi{i}", [64, 512], I32) for i in range(1)]
        ang = [P.sb(f"ang{i}", [64, 512], F32) for i in range(1)]
        rti = [P.sb(f"rti{i}", [64, 512], I32) for i in range(1)]
        rtf = [P.sb(f"rtf{i}", [64, 512], F32) for i in range(1)]
        cs = [P.sb(f"cs{i}", [64, 2, 512], F32) for i in range(1)]
        gt = [P.sb(f"gt{i}", [128, 512], BF16) for i in range(4)]
        banks = [P.ps(f"b{i}", [128, 512], F32) for i in range(6)]
        tpb = [P.ps(f"tp{i}", [128, 8, 128], BF16) for i in range(2)]
        bank_rr = RR(range(6))
        evac_rr = RR(["vector", "scalar"])
        gt_rr = RR(range(4))
        dq_rr = RR(["sync", "gpsimd"])

        def load_x(blk):
            b2 = blk % 2
            P.dma("sync", xs[b2][:], x[blk * 512:(blk + 1) * 512, :].rearrange("(j p) d -> p j d", p=128), writes=[f"xs{b2}"], key=f"xs{b2}")

        def rope_tables(blk):
            c0, c1 = blk * 512, (blk + 1) * 512
            PS, ANG, RTI, RTF, CS = posi[0], ang[0], rti[0], rtf[0], cs[0]
            rk = "rr0"
            P.dma("sync", PS[:], pos[c0:c1].partition_broadcast(64), writes=["posi0"], key="posi0")
            P.copy("gpsimd", ANG[:], PS[:], ["posi0"], [rk + "_ang"])
            P.ts("gpsimd", ANG[:], ANG[:], invf[:, 0:1], 0.0, ALU.mult, ALU.add, [rk + "_ang", "invf"], [rk + "_ang"])
            range_reduce_sin(P, "gpsimd", ANG[:], CS[:, 1, :], RTI[:], RTF[:], rk, ("cs0", 1), shift=0.0)
            P.act(ANG[:], RTF[:], AF.Abs, [rk + "_tf", rk + "_ang"], [rk + "_ang"])
            P.act(CS[:, 0, :], ANG[:], AF.Sin, [rk + "_ang"], [("cs0", 0)], scale=-1.0, bias=PI / 2)
            P.dma("sync", rope_s[:, :, c0:c1].rearrange("a p t -> p a t"), CS[:], reads=[("cs0", 0), ("cs0", 1)], key="cs0")

        def front(blk):
            sup, half = blk // 2, blk % 2
            b2 = blk % 2
            c0, c1 = blk * 512, (blk + 1) * 512
            b1 = 0
            X, XN, XT = xs[b2], xn[b1], xnT[sup % 2]
            kx, kxn, kxt = f"xs{b2}", f"xn{b1}", (f"xnT{sup % 2}", half)
            for j in range(4):
                col = blk * 4 + j
                P.act(junk[j][:], X[:, j, :], AF.Square, [kx], [f"junk{j}", ("ssx", blk, j)], accum_out=ssx[:, col:col + 1])
            P.act(inv[:, blk * 4:blk * 4 + 4], ssx[:, blk * 4:blk * 4 + 4], AF.Sqrt, [("ssx", blk, j) for j in range(4)], [("inv", blk)],
                  scale=1.0 / D, bias=EPS)
            P.op("vector", "reciprocal", [("inv", blk)], [("inv", blk)], out=inv[:, blk * 4:blk * 4 + 4], in_=inv[:, blk * 4:blk * 4 + 4])
            for j in range(4):
                col = blk * 4 + j
                P.stt(XN[:, j, :], X[:, j, :], inv[:, col:col + 1], gmix[:], ALU.mult, ALU.mult,
                      [kx, ("inv", blk), "gmix"], [(kxn, j)])
                tp, ktp = tpb[j % 2], f"tp{j % 2}"
                for k in range(8):
                    P.tr(tp[:, k, :], XN[:, j, k * 128:(k + 1) * 128], ident[:], [(kxn, j), "ident"], [ktp])
                o0 = half * 512 + j * 128
                P.copy(evac_rr(), XT[:, :, o0:o0 + 128], tp[:, :, :], [ktp], [kxt])

        load_x(0)
        rope_tables(0)
        front(0)
        for blk in range(NB):
            sup, half = blk // 2, blk % 2
            b2 = blk % 2
            c0, c1 = blk * 512, (blk + 1) * 512
            b1 = 0
            if blk + 1 < NB:
                load_x(blk + 1)
            X, XN, XT = xs[b2], xn[b1], xnT[sup % 2]
            kx, kxn, kxt = f"xs{b2}", f"xn{b1}", (f"xnT{sup % 2}", half)

            def proj(col0, m, bank):
                for k in range(8):
                    P.mm(banks[bank][0:m, :], win[:, k, col0:col0 + m], XT[:, k, half * 512:(half + 1) * 512],
                         (k == 0), (k == 7), [kxt, ("win", k)], [("bank", bank)])

            QL, SQ, RQ, QLO = ql[b1], sq[b1], rq[b1], qlo[b1]
            for i in range(5):
                bk = bank_rr()
                proj(512 + i * 128, 128, bk)
                P.copy("scalar", QL[:, i, :], banks[bk][:, :], [("bank", bk)], [(f"ql{b1}", i)])
                P.tt("vector", SQ[:, i, :], banks[bk][:, :], QL[:, i, :], ALU.mult, [("bank", bk), (f"ql{b1}", i)], [(f"sq{b1}", i)])
            for grp, (i0, i1, dim) in enumerate([(0, 3, 384), (3, 5, 256)]):
                bk = bank_rr()
                for i in range(i0, i1):
                    P.mm(banks[bk][:, :], ones_bf[:], SQ[:, i, :], (i == i0), (i == i1 - 1), [(f"sq{b1}", i), "ones"], [("bank", bk)])
                P.act(RQ[:, grp, :], banks[bk][:, :], AF.Sqrt, [("bank", bk)], [(f"rq{b1}", grp)], scale=1.0 / dim, bias=EPS)
                P.op("vector", "reciprocal", [(f"rq{b1}", grp)], [(f"rq{b1}", grp)], out=RQ[:, grp, :], in_=RQ[:, grp, :])
                for i in range(i0, i1):
                    P.stt(QLO[:, i, :], QL[:, i, :], gq[:, i, :], RQ[:, grp, :], ALU.mult, ALU.mult,
                          [(f"ql{b1}", i), (f"rq{b1}", grp), "gq"], [(f"qlo{b1}", i)])
            P.dma("sync", qln_s[:, c0:c1].rearrange("(i p) t -> p i t", p=128), QLO[:, 0:3, :],
                  reads=[(f"qlo{b1}", i) for i in range(3)], key=f"qlo{b1}")
            P.dma("sync", ckvn_s[:, c0:c1].rearrange("(i p) t -> p i t", p=128), QLO[:, 3:5, :],
                  reads=[(f"qlo{b1}", i) for i in range(3, 5)], key=f"qlo{b1}")
            PS, ANG, RTI, RTF, CS = posi[0], ang[0], rti[0], rtf[0], cs[0]
            bk = bank_rr()
            proj(1152, 64, bk)
            KP, KG, SQK = kp[b1], kg[b1], sqk[b1]
            P.copy("scalar", KP[:], banks[bk][0:64, :], [("bank", bk)], [f"kp{b1}"])
            P.tt("vector", SQK[:], KP[:], KP[:], ALU.mult, [f"kp{b1}"], [f"sqk{b1}"])
            P.ts("vector", KG[:], KP[:], gkr[:, 0:1], None, ALU.mult, None, [f"kp{b1}", "gkr"], [f"kg{b1}"])
            bk = bank_rr()
            P.mm(banks[bk][:, :], ones_bf[0:64, :], SQK[:], True, True, [f"sqk{b1}", "ones"], [("bank", bk)])
            P.copy("scalar", ssko[b1][:], banks[bk][:, :], [("bank", bk)], [f"ssko{b1}"])
            P.dma("sync", sskpe_s[:, c0:c1], ssko[b1][:], reads=[f"ssko{b1}"], key=f"ssko{b1}")
            bk = bank_rr()
            P.mm(banks[bk][0:64, :], rm[:], KG[:], True, True, [f"kg{b1}", "rm"], [("bank", bk)])
            P.tt("vector", kt1[b1][:], KG[:], CS[:, 0, :], ALU.mult, [f"kg{b1}", (f"cs{b1}", 0)], [f"kt1{b1}"])
            P.tt("vector", kt2[b1][:], banks[bk][0:64, :], CS[:, 1, :], ALU.mult, [("bank", bk), (f"cs{b1}", 1)], [f"kt2{b1}"])
            P.tt("vector", kpo[b1][:], kt1[b1][:], kt2[b1][:], ALU.add, [f"kt1{b1}", f"kt2{b1}"], [f"kpo{b1}"])
            P.dma("sync", kper_s[:, c0:c1], kpo[b1][:], reads=[f"kpo{b1}"], key=f"kpo{b1}")
            if blk + 1 < NB:
                front(blk + 1)
            for ft in range(16):
                bk = bank_rr()
                proj(1216 + ft * 128, 128, bk)
                gi = gt_rr()
                P.act(gt[gi][:], banks[bk][:, :], AF.Sigmoid, [("bank", bk)], [f"gt{gi}"])
                P.dma("sync" if gi % 2 == 0 else "gpsimd", gates_s[ft * 128:(ft + 1) * 128, c0:c1], gt[gi][:], reads=[f"gt{gi}"], key=f"gt{gi}")
            if half == 1:
                kxt2 = [(f"xnT{sup % 2}", 0), (f"xnT{sup % 2}", 1)]
                for s in range(8):
                    bk = bank_rr()
                    for k in range(8):
                        P.mm(banks[bk][:, :], XT[:, k, s:1024:8], win[:, k, 0:512], (k == 0), (k == 7),
                             kxt2 + [("win", k)], [("bank", bk)])
                    P.copy(evac_rr(), u8[:, :, s, :], banks[bk][:, :].rearrange("p (g c) -> p g c", c=16), [("bank", bk)], [("u8", s)])
                for gg in range(4):
                    tp, ktp = tpb[gg % 2], f"tp{gg % 2}"
                    for gi in range(8):
                        g = gg * 8 + gi
                        P.tr(tp[:, gi, :], u8[:, g, :, :].rearrange("p s c -> p (s c)"), ident[:],
                             [("u8", s) for s in range(8)] + ["ident"], [ktp])
                    UT = u8T[gg % 2]
                    P.copy(evac_rr(), UT[:], tp[:, :, :], [ktp], [f"u8T{gg % 2}"])
                    P.dma("sync" if gg % 2 == 0 else "gpsimd", u8_s[:, gg * 8:(gg + 1) * 8, sup * 128:(sup + 1) * 128], UT[:], reads=[f"u8T{gg % 2}"],
                          key=f"u8T{gg % 2}")
            if blk + 1 < NB:
                rope_tables(blk + 1)
    if stop_after == 1:
        gstack.close()
        return nc, dbg

    with Phase(nc, "pa", G) as P:
        if pad:
            P.sb("pad", [128, pad], F32)
        qlnT = P.sb("qlnT", [128, 3, L], BF16)
        ckvnT = P.sb("ckvnT", [128, 2, L], BF16)
        kper = P.sb("kper", [64, L], BF16)
        sskpe = P.sb("sskpe", [128, L], F32)
        for blk in range(NB):
            c0, c1 = blk * 512, (blk + 1) * 512
            P.dma("sync", qlnT[:, :, c0:c1], qln_s[:, c0:c1].rearrange("(i p) t -> p i t", p=128), writes=["lat"], key="lat")
            P.dma("sync", ckvnT[:, :, c0:c1], ckvn_s[:, c0:c1].rearrange("(i p) t -> p i t", p=128), writes=["lat"], key="lat")
            P.dma("sync", kper[:, c0:c1], kper_s[:, c0:c1], writes=["lat"], key="lat")
            P.dma("sync", sskpe[:, c0:c1], sskpe_s[:, c0:c1], writes=["lat"], key="lat")
        wqb = P.sb("wqb", [128, 3, 1536], BF16)
        wkvb = P.sb("wkvb", [128, 2, 2048], BF16)
        cast_load(P, wqb, w_q_b, 3, "wqb", "wqb")
        cast_load(P, wkvb, w_kv_b, 2, "wkvb", "wkvb")
        WQB = [("wqb", k) for k in range(3)]
        WKVB = [("wkvb", k) for k in range(2)]
        gqn = P.sb("gqn", [128, 2], F32)
        gkn = P.sb("gkn", [128, 1], F32)
        P.dma("sync", gqn[:, 0:1], bass.AP(q_norm.tensor, 0, [[1, 128], [1, 1]]), writes=["gqn"], key="gqn")
        P.dma("sync", gqn[0:64, 1:2], bass.AP(q_norm.tensor, 128, [[1, 64], [1, 1]]), writes=["gqn"], key="gqn")
        P.dma("sync", gkn[:, 0:1], bass.AP(k_norm.tensor, 0, [[1, 128], [1, 1]]), writes=["gkn"], key="gkn")
        identf = P.sb("identf", [128, 128], F32)
        ident = P.sb("ident", [128, 128], BF16)
        ones_bf = P.sb("ones", [128, 128], BF16)
        make_ident(P, identf, ident)
        P.op("gpsimd", "memset", [], ["ones"], ap=ones_bf[:], constant=1.0)
        rm = P.sb("rm", [64, 64], F32)
        rm2 = P.sb("rm2", [64, 64], F32)
        make_rm(P, rm, rm2)
        NT = L // 128
        QnT = [P.sb(f"QnT{i}", [128, L], BF16) for i in range(2)]
        QrT = [P.sb(f"QrT{i}", [128, L], BF16) for i in range(2)]
        KnT = [P.sb(f"KnT{i}", [128, L], BF16) for i in range(2)]
        KrT = [P.sb(f"KrT{i}", [128, L], BF16) for i in range(2)]
        V = [P.sb(f"V{i}", [128, NT, 130], BF16) for i in range(2)]
        for i in range(2):
            P.op("gpsimd", "memset", [], [("Vones", i)], ap=V[i][:, :, 128:130], constant=1.0)
            P.op("gpsimd", "memset", [], [("Qz", i)], ap=QrT[i][64:128, :], constant=0.0)
            P.op("gpsimd", "memset", [], [("Kz", i)], ap=KrT[i][64:128, :], constant=0.0)
        sqa = P.sb("sqa", [128, 512], BF16)
        sqb = P.sb("sqb", [128, 512], BF16)
        P.op("gpsimd", "memset", [], ["sqb_hi"], ap=sqb[64:128, :], constant=0.0)
        rq = P.sb("rq", [128, 512], F32)
        rq2 = P.sb("rq2", [128, 512], F32)
        qr = P.sb("qr", [64, 512], F32)
        t1 = P.sb("t1", [64, 512], F32)
        t2 = P.sb("t2", [64, 512], F32)
        CS = P.sb("cs", [64, 2, 512], F32)
        pT = [P.sb(f"pT{i}", [128, 512], BF16) for i in range(4)]
        atT = [P.sb(f"atT{i}", [128, 512], BF16) for i in range(2)]
        acc = [P.sb(f"acc{i}", [128, 512], F32) for i in range(2)]
        rden = P.sb("rden", [128, 512], F32)
        ones_f = P.sb("ones_f", [128, 128], F32)
        P.op("gpsimd", "memset", [], ["ones_f"], ap=ones_f[:], constant=1.0)
        pb = [P.ps(f"pb{i}", [128, 512], F32) for i in range(2)]
        oT = [P.ps(f"oT{i}", [128, 512], F32) for i in range(2)]
        dn = P.ps("dn", [128, 512], F32)
        st = [P.ps(f"st{i}", [128, 512], F32) for i in range(3)]
        st_rr, pb_rr, pT_rr, atT_rr, oT_rr = RR(range(3)), RR(range(2)), RR(range(4)), RR(range(2)), RR(range(2))
        ev_rr = RR(["vector", "scalar"])
        SCALE = 192.0 ** -0.5

        def prep(h, blk):
            hb = h % 2
            c0, c1 = blk * 512, (blk + 1) * 512
            P.dma("sync", CS[:], rope_s[:, :, c0:c1].rearrange("a p t -> p a t"), writes=["cs"], key="cs")
            b1 = pb_rr()
            for k in range(3):
                P.mm(pb[b1][:, :], wqb[:, k, h * 192:h * 192 + 128], qlnT[:, k, c0:c1], k == 0, k == 2,
                     [("wqb", k), "lat"], [("pb", b1)])
            P.act(sqa[:], pb[b1][:, :], AF.Square, [("pb", b1)], ["sqa"])
            b2 = pb_rr()
            for k in range(3):
                P.mm(pb[b2][0:64, :], wqb[:, k, h * 192 + 128:h * 192 + 192], qlnT[:, k, c0:c1], k == 0, k == 2,
                     [("wqb", k), "lat"], [("pb", b2)])
            P.act(sqb[0:64, :], pb[b2][0:64, :], AF.Square, [("pb", b2)], ["sqb"])
            P.ts("vector", qr[:], pb[b2][0:64, :], gqn[0:64, 1:2], None, ALU.mult, None, [("pb", b2), "gqn"], ["qr"])
            yield
            b3 = pb_rr()
            P.mm(pb[b2][:, :], ones_bf[:, :], sqa[:], True, False, ["ones", "sqa"], [("pb", b2)])
            P.mm(pb[b2][:, :], ones_bf[:, :], sqb[:], False, True, ["ones", "sqb", "sqb_hi"], [("pb", b2)])
            yield
            P.act(rq[:], pb[b2][:, :], AF.Ln, [("pb", b2)], ["rq"], scale=1.0 / 192, bias=EPS)
            P.act(rq[:], rq[:], AF.Exp, ["rq"], ["rq"], scale=-0.5)
            P.stt(QnT[hb][:, c0:c1], pb[b1][:, :], gqn[:, 0:1], rq[:], ALU.mult, ALU.mult, [("pb", b1), "gqn", "rq"], [("QnT", hb, blk)])
            P.tt("vector", qr[:], qr[:], rq[0:64, :], ALU.mult, ["qr", "rq"], ["qr"])
            yield
            P.mm(pb[b3][0:64, :], rm[:], qr[:], True, True, ["rm", "qr"], [("pb", b3)])
            yield
            P.tt("vector", t1[:], qr[:], CS[:, 0, :], ALU.mult, ["qr", "cs"], ["t1"])
            P.tt("vector", t2[:], pb[b3][0:64, :], CS[:, 1, :], ALU.mult, [("pb", b3), "cs"], ["t2"])
            P.tt("vector", QrT[hb][0:64, c0:c1], t1[:], t2[:], ALU.add, ["t1", "t2"], [("QrT", hb, blk)])
            yield
            b4 = pb_rr()
            for k in range(2):
                P.mm(pb[b4][:, :], wkvb[:, k, h * 256:h * 256 + 128], ckvnT[:, k, c0:c1], k == 0, k == 1,
                     [("wkvb", k), "lat"], [("pb", b4)])
            P.act(sqa[:], pb[b4][:, :], AF.Square, [("pb", b4)], ["sqa"])
            yield
            b5 = pb_rr()
            P.mm(pb[b5][:, :], ones_bf[:, :], sqa[:], True, True, ["ones", "sqa"], [("pb", b5)])
            yield
            P.tt("vector", rq2[:], pb[b5][:, :], sskpe[:, c0:c1], ALU.add, [("pb", b5), "lat"], ["rq2"])
            P.act(rq2[:], rq2[:], AF.Ln, ["rq2"], ["rq2"], scale=1.0 / 192, bias=EPS)
            P.act(rq2[:], rq2[:], AF.Exp, ["rq2"], ["rq2"], scale=-0.5)
            P.stt(KnT[hb][:, c0:c1], pb[b4][:, :], gkn[:, 0:1], rq2[:], ALU.mult, ALU.mult, [("pb", b4), "gkn", "rq2"], [("KnT", hb, blk)])
            P.tt("vector", KrT[hb][0:64, c0:c1], kper[:, c0:c1], rq2[0:64, :], ALU.mult, ["lat", "rq2"], [("KrT", hb, blk)])
            yield
            b6 = pb_rr()
            for j in range(4):
                for k in range(2):
                    P.mm(pb[b6][:, j * 128:(j + 1) * 128], ckvnT[:, k, c0 + j * 128:c0 + (j + 1) * 128],
                         wkvb[:, k, h * 256 + 128:h * 256 + 256], k == 0, k == 1, [("wkvb", k), "lat"], [("pb", b6)])
            P.copy(ev_rr(), V[hb][:, blk * 4:(blk + 1) * 4, 0:128], pb[b6][:, :].rearrange("p (j d) -> p j d", d=128),
                   [("pb", b6)], [("V", hb, blk)])

        def attn(h, qb, gen):
            hb = h % 2
            nkt = 4 * qb + 4
            o_ = oT_rr()

            def qk(kt):
                i = kt - 4 * qb
                cq0 = 128 * i if i > 0 else 0
                kblk = kt // 4
                s_ = st_rr()
                P.mm(st[s_][:, cq0:512], KnT[hb][:, kt * 128:(kt + 1) * 128], QnT[hb][:, qb * 512 + cq0:(qb + 1) * 512], True, False,
                     [("KnT", hb, kblk), ("QnT", hb, qb)], [("st", s_)])
                P.mm(st[s_][:, cq0:512], KrT[hb][:, kt * 128:(kt + 1) * 128], QrT[hb][:, qb * 512 + cq0:(qb + 1) * 512], False, True,
                     [("KrT", hb, kblk), ("QrT", hb, qb), ("Qz", hb), ("Kz", hb)], [("st", s_)])
                pi = pT_rr()
                P.act(pT[pi][:, cq0:512], st[s_][:, cq0:512], AF.Exp, [("st", s_)], [("pT", pi)], scale=SCALE)
                if i >= 0:
                    P.op("gpsimd", "affine_select", [("pT", pi)], [("pT", pi)], out=pT[pi][:, cq0:cq0 + 128], in_=pT[pi][:, cq0:cq0 + 128],
                         pattern=[[1, 128]], compare_op=ALU.is_ge, fill=0.0, base=0, channel_multiplier=-1)
                return pi, cq0

            def pv(kt, pi, cq0):
                kblk = kt // 4
                P.mm(oT[o_][:, cq0:512], V[hb][:, kt, 0:128], pT[pi][:, cq0:512], kt == 0, kt == nkt - 1,
                     [("pT", pi), ("V", hb, kblk)], [("oT", o_)])
                a = 1 if kt % 4 == 3 else 0
                eng = "vector" if a == 0 else "gpsimd"
                if kt == 0:
                    P.copy(eng, acc[0][:, :], pT[pi][:, :], [("pT", pi)], [("acc", 0)])
                else:
                    P.tt(eng, acc[a][:, cq0:512], acc[a][:, cq0:512], pT[pi][:, cq0:512], ALU.add, [("pT", pi), ("acc", a)], [("acc", a)])

            P.op("gpsimd", "memset", [], [("acc", 1)], ap=acc[1][:], constant=0.0)
            pend = [qk(0), qk(1)]
            for kt in range(nkt):
                if kt + 2 < nkt:
                    pend.append(qk(kt + 2))
                pv(kt, *pend.pop(0))
                if gen is not None and kt % 2 == 1:
                    next(gen, None)
            P.tt("vector", acc[0][:, :], acc[0][:, :], acc[1][:, :], ALU.add, [("acc", 0), ("acc", 1)], [("acc", 0)])
            P.mm(dn[:, :], ones_f[:, :], acc[0][:, :], True, True, ["ones_f", ("acc", 0)], [("dn", 0)])
            P.act(rden[:, :], dn[:, :], AF.Ln, [("dn", 0)], ["rden"])
            P.act(rden[:, :], rden[:, :], AF.Exp, ["rden"], ["rden"], scale=-1.0)
            ai = atT_rr()
            P.tt("vector", atT[ai][:, :], oT[o_][:, :], rden[:, :], ALU.mult, [("oT", o_), "rden"], [("atT", ai)])
            P.dma("sync", attnT_s[h * 128:(h + 1) * 128, qb * 512:(qb + 1) * 512], atT[ai][:], reads=[("atT", ai)], key=f"atT{ai}")

        for blk in range(NB):
            for _ in prep(0, blk):
                pass
        def prep_all(h):
            for b in range(NB):
                yield from prep(h, b)

        for h in range(8):
            gen = prep_all(h + 1) if h + 1 < 8 else None
            for qb in range(NB):
                attn(h, qb, gen)
            if gen is not None:
                for _ in gen:
                    pass
    if stop_after == 2:
        gstack.close()
        return nc, dbg

    def bcast_last(ap, n):
        dims = [list(d) for d in ap.ap]
        assert dims[-1][1] == 1
        dims[-1] = [0, n]
        return bass.AP(ap.tensor, ap.offset, dims)

    if "nossm" not in debug:
        NCT = NC // 128
        sstack = contextlib.ExitStack()
        sb_p = lambda name, shape, dt: sstack.enter_context(nc.sbuf_tensor("ss_" + name, list(shape), dt))
        Wb = [sb_p(f"Wb{i}", [128, 16, 128], BF16) for i in range(2)]
        Toep = sb_p("Toep", [128, 32, 128], BF16)
        Cz = [sb_p(f"Cz{i}", [128, 32, 128], BF16) for i in range(2)]
        ar8 = sb_p("ar8", [128, 16], F32)
        th8 = sb_p("th8", [128, 16], F32)
        with Phase(nc, "s0", G) as P:
            if pad:
                P.sb("pad", [128, pad], F32)
            are = P.sb("are", [128, 16, 1], F32)
            aim = P.sb("aim", [128, 16, 1], F32)
            ldt = P.sb("ldt", [128, 16, 1], F32)
            bre = P.sb("bre", [128, 16, 16], F32)
            bim = P.sb("bim", [128, 16, 16], F32)
            cre = P.sb("cre", [128, 16, 16, 1], F32)
            cim = P.sb("cim", [128, 16, 16, 1], F32)
            dbc = P.sb("dbc", [128, 32, 16], F32)
            for gg in range(2):
                pr = slice(gg * 64, (gg + 1) * 64)
                P.dma("sync", are[pr, :, :], bass.AP(ssm_a_re.tensor, gg * 64, [[1, 64], [128, 16], [1, 1]]), writes=["are"], key="ld_are",
                      allow_slow_non_contiguous=True)
                P.dma("sync", aim[pr, :, :], bass.AP(ssm_a_im.tensor, gg * 64, [[1, 64], [128, 16], [1, 1]]), writes=["aim"], key="ld_aim",
                      allow_slow_non_contiguous=True)
                P.dma("sync", ldt[pr, :, :], bass.AP(ssm_log_dt.tensor, gg, [[0, 64], [2, 16], [1, 1]]), writes=["ldt"], key="ld_ldt",
                      allow_slow_non_contiguous=True)
                P.dma("sync", bre[pr, :, :], bass.AP(ssm_b_re.tensor, gg * 1024, [[16, 64], [2048, 16], [1, 16]]), writes=["bre"], key="ld_bre")
                P.dma("sync", bim[pr, :, :], bass.AP(ssm_b_im.tensor, gg * 1024, [[16, 64], [2048, 16], [1, 16]]), writes=["bim"], key="ld_bim")
            P.dma("sync", dbc[:].rearrange("p g c -> p (g c)"), ssm_d.rearrange("g c -> (g c)").partition_broadcast(128), writes=["dbc"], key="ld_dbc")
            LD = ["are", "aim", "ldt", "bre", "bim", "cre", "cim", "dbc"]
            identf = P.sb("identf", [128, 128], F32)
            ident = P.sb("ident", [128, 128], BF16)
            make_ident(P, identf, ident)
            cnat = [P.sb(f"cnat{i}", [128, 4, 2, 64], F32) for i in range(2)]
            ctp = [P.ps(f"ps{i}", [128, 512], F32) for i in range(4)]
            for i, src in enumerate((ssm_c_re, ssm_c_im)):
                for dup in range(2):
                    P.dma("sync", cnat[i][:, :, dup, :], bass.AP(src.tensor, 0, [[64, 128], [8192, 4], [1, 64]]), writes=[f"cnat{i}"], key=f"ld_cnat{i}")
                for t in range(4):
                    P.tr(ctp[i * 2 + t // 2][:, (t % 2) * 128:(t % 2) * 128 + 128], cnat[i][:, t, :, :].rearrange("p d q -> p (d q)"), identf[:],
                         [f"cnat{i}", "identf"], [("ps", i * 2 + t // 2)])
                dst = (cre, cim)[i]
                for t in range(4):
                    for gg in range(2):
                        pr = slice(gg * 64, (gg + 1) * 64)
                        src_v = ctp[i * 2 + t // 2][pr, (t % 2) * 128:(t % 2) * 128 + 128].rearrange("p (jj g c) -> p jj g c", g=2, c=16)[:, :, gg, :]
                        P.copy("vector" if gg == 0 else "scalar", dst[pr, t * 4:(t + 1) * 4, :, :].rearrange("p j c o -> p j (c o)"), src_v,
                               [("ps", i * 2 + t // 2)], [("cre", "cim")[i]])
            dtt = P.sb("dtt", [128, 16], F32)
            ar = P.sb("ar", [128, 16], F32)
            th = P.sb("th", [128, 16], F32)
            A2 = lambda t: t[:].rearrange("p j o -> p (j o)")
            P.act(dtt[:], A2(ldt), AF.Exp, LD, ["dtt"])
            P.tt("vector", ar[:], A2(are), dtt[:], ALU.mult, LD + ["dtt"], ["ar"])
            P.tt("vector", th[:], A2(aim), dtt[:], ALU.mult, LD + ["dtt"], ["th"])
            P.ts("vector", ar8[:], ar[:], 8.0, None, ALU.mult, None, ["ar"], ["ar8"])
            P.ts("vector", th8[:], th[:], 8.0, None, ALU.mult, None, ["th"], ["th8"])
            kki = P.sb("kki", [128, 16], I32)
            kk = P.sb("kk", [128, 16], F32)
            P.op("gpsimd", "iota", [], ["kki"], out=kki[:], pattern=[[1, 16]], base=-7, channel_multiplier=0)
            P.copy("gpsimd", kk[:], kki[:], ["kki"], ["kk"])
            mag = P.sb("mag", [128, 16, 16], F32)
            ang = P.sb("ang", [128, 16, 16], F32)
            rti = P.sb("rti", [128, 256], I32)
            rtf = P.sb("rtf", [128, 256], F32)
            sn = P.sb("sn", [128, 16, 16], F32)
            cs = P.sb("cs", [128, 16, 16], F32)
            LRE = P.sb("LRE", [128, 16, 16], F32)
            LIM = P.sb("LIM", [128, 16, 16], F32)
            for j in range(16):
                P.act(mag[:, j, :], kk[:], AF.Exp, ["kk", "ar"], [("mag", j)], scale=ar[:, j:j + 1])
                P.ts("gpsimd", ang[:, j, :], kk[:], th[:, j:j + 1], None, ALU.mult, None, ["kk", "th"], ["rr_ang"])
            F2 = lambda t: t[:].rearrange("p j k -> p (j k)")
            range_reduce_sin(P, "gpsimd", F2(ang), F2(cs), rti[:], rtf[:], "rr", "cs", shift=PI / 2)
            range_reduce_sin(P, "gpsimd", F2(ang), F2(sn), rti[:], rtf[:], "rr", "sn", shift=0.0)
            MAG = [("mag", j) for j in range(16)]
            P.tt("vector", F2(LRE), F2(mag), F2(cs), ALU.mult, MAG + ["cs"], ["LRE"])
            P.tt("vector", F2(LIM), F2(mag), F2(sn), ALU.mult, MAG + ["sn"], ["LIM"])
            nr = P.sb("nr", [128, 16], F32)
            ni = P.sb("ni", [128, 16], F32)
            den = P.sb("den", [128, 16], F32)
            tq1 = P.sb("tq1", [128, 16], F32)
            tq2 = P.sb("tq2", [128, 16], F32)
            bere = P.sb("bere", [128, 16], F32)
            beim = P.sb("beim", [128, 16], F32)
            P.ts("vector", nr[:], LRE[:, :, 8], -1.0, None, ALU.add, None, ["LRE"], ["nr"])
            P.copy("vector", ni[:], LIM[:, :, 8], ["LIM"], ["ni"])
            P.tt("vector", den[:], A2(are), A2(are), ALU.mult, LD, ["den"])
            P.tt("vector", tq1[:], A2(aim), A2(aim), ALU.mult, LD, ["tq1"])
            P.tt("vector", den[:], den[:], tq1[:], ALU.add, ["den", "tq1"], ["den"])
            P.op("vector", "reciprocal", ["den"], ["den"], out=den[:], in_=den[:])
            P.tt("vector", tq1[:], nr[:], A2(are), ALU.mult, ["nr"] + LD, ["tq1"])
            P.tt("vector", tq2[:], ni[:], A2(aim), ALU.mult, ["ni"] + LD, ["tq2"])
            P.tt("vector", tq1[:], tq1[:], tq2[:], ALU.add, ["tq1", "tq2"], ["tq1"])
            P.tt("vector", bere[:], tq1[:], den[:], ALU.mult, ["tq1", "den"], ["bere"])
            P.tt("vector", tq1[:], ni[:], A2(are), ALU.mult, ["ni"] + LD, ["tq1"])
            P.tt("vector", tq2[:], nr[:], A2(aim), ALU.mult, ["nr"] + LD, ["tq2"])
            P.tt("vector", tq1[:], tq1[:], tq2[:], ALU.subtract, ["tq1", "tq2"], ["tq1"])
            P.tt("vector", beim[:], tq1[:], den[:], ALU.mult, ["tq1", "den"], ["beim"])
            Bre = P.sb("Bre", [128, 16, 16], F32)
            Bim = P.sb("Bim", [128, 16, 16], F32)
            u1 = P.sb("u1", [128, 16, 16], F32)
            u2 = P.sb("u2", [128, 16, 16], F32)
            bb_re = bcast_last(bere[:].rearrange("p (j o) -> p j o", o=1), 16)
            bb_im = bcast_last(beim[:].rearrange("p (j o) -> p j o", o=1), 16)
            P.tt("vector", u1[:], bre[:], bb_re, ALU.mult, LD + ["bere"], ["u1"])
            P.tt("vector", u2[:], bim[:], bb_im, ALU.mult, LD + ["beim"], ["u2"])
            P.tt("vector", Bre[:], u1[:], u2[:], ALU.subtract, ["u1", "u2"], ["Bre"])
            P.tt("vector", u1[:], bim[:], bb_re, ALU.mult, LD + ["bere"], ["u1"])
            P.tt("vector", u2[:], bre[:], bb_im, ALU.mult, LD + ["beim"], ["u2"])
            P.tt("vector", Bim[:], u1[:], u2[:], ALU.add, ["u1", "u2"], ["Bim"])
            Pw = [P.sb(f"Pw{i}", [128, 16, 8, 16], F32) for i in range(2)]
            Pm = [P.sb(f"Pm{i}", [128, 16, 8, 16], F32) for i in range(2)]
            Qre = P.sb("Qre", [128, 16, 9, 16], F32)
            Qin = P.sb("Qin", [128, 16, 9, 16], F32)
            v1 = [P.sb(f"v1{i}", [128, 16, 16], F32) for i in range(2)]
            v2 = [P.sb(f"v2{i}", [128, 16, 16], F32) for i in range(2)]
            eng_rr = RR(["vector", "gpsimd"])

            def cmul(out_re, out_im, are_, aim_, kidx, rk, wk_re, wk_im, neg_im=False):
                lr = bcast_last(LRE[:, :, kidx:kidx + 1], 16)
                li = bcast_last(LIM[:, :, kidx:kidx + 1], 16)
                e = eng_rr()
                i = 0 if e == "vector" else 1
                P.tt(e, v1[i][:], are_, lr, ALU.mult, rk + ["LRE"], [f"v1{i}"])
                P.tt(e, v2[i][:], aim_, li, ALU.mult, rk + ["LIM"], [f"v2{i}"])
                P.tt(e, out_re, v1[i][:], v2[i][:], ALU.subtract, [f"v1{i}", f"v2{i}"], [wk_re])
                P.tt(e, v1[i][:], are_, li, ALU.mult, rk + ["LIM"], [f"v1{i}"])
                P.tt(e, v2[i][:], aim_, lr, ALU.mult, rk + ["LRE"], [f"v2{i}"])
                if neg_im:
                    P.stt(out_im, v1[i][:], -1.0, v2[i][:], ALU.mult, ALU.subtract, [f"v1{i}", f"v2{i}"], [wk_im]) if e == "vector" else (
                        P.tt(e, v1[i][:], v1[i][:], v2[i][:], ALU.add, [f"v1{i}", f"v2{i}"], [f"v1{i}"]),
                        P.ts(e, out_im, v1[i][:], -1.0, None, ALU.mult, None, [f"v1{i}"], [wk_im]))
                else:
                    P.tt(e, out_im, v1[i][:], v2[i][:], ALU.add, [f"v1{i}", f"v2{i}"], [wk_im])

            for s_ in range(8):
                cmul(Pw[0][:, :, s_, :], Pw[1][:, :, s_, :], Bre[:], Bim[:], 14 - s_, ["Bre", "Bim"], ("Pw0", s_), ("Pw1", s_))
                cmul(Pm[0][:, :, s_, :], Pm[1][:, :, s_, :], Bre[:], Bim[:], 7 - s_, ["Bre", "Bim"], ("Pm0", s_), ("Pm1", s_))
            C3 = lambda t: t[:].rearrange("p j c o -> p j (c o)")
            for s_ in range(9):
                cmul(Qre[:, :, s_, :], Qin[:, :, s_, :], C3(cre), C3(cim), 7 + s_, LD, ("Qre", s_), ("Qin", s_), neg_im=True)
            PW = [[(f"Pw{i}", s_) for s_ in range(8)] for i in range(2)]
            PM = [[(f"Pm{i}", s_) for s_ in range(8)] for i in range(2)]
            QRE = [("Qre", s_) for s_ in range(9)]
            QIN = [("Qin", s_) for s_ in range(9)]
            pbk = ctp
            pb_rr = RR(range(4))
            ev_rr = RR(["vector", "scalar"])
            for i in range(2):
                for jq in range(4):
                    bk = pb_rr()
                    for jj in range(4):
                        j = jq * 4 + jj
                        P.tr(pbk[bk][:, jj * 128:(jj + 1) * 128], Pw[i][:, j, :, :].rearrange("p s c -> p (s c)"), identf[:],
                             PW[i] + ["identf"], [("ps", bk)])
                    P.copy(ev_rr(), Wb[i][:, jq * 4:(jq + 1) * 4, :], pbk[bk][:, :].rearrange("p (j n) -> p j n", n=128), [("ps", bk)], [f"Wb{i}"])
            mask = P.sb("mask", [128, 8, 16], F32)
            P.op("gpsimd", "memset", [], ["mask"], ap=mask[:], constant=1.0)
            P.op("gpsimd", "affine_select", ["mask"], ["mask"], out=mask[:], in_=mask[:], pattern=[[16, 8], [0, 16]],
                 compare_op=ALU.is_ge, fill=0.0, base=15, channel_multiplier=-1)
            tz1 = [P.sb(f"tz1{i}", [128, 128], F32) for i in range(2)]
            tz2 = [P.sb(f"tz2{i}", [128, 128], F32) for i in range(2)]
            MK = mask[:].rearrange("p s c -> p (s c)")
            for z_ in range(2):
                P.op("gpsimd", "memset", [], [f"Cz{z_}"], ap=Cz[z_][:], constant=0.0)
            for g in range(32):
                j, gg = g // 2, g % 2
                pr = slice(gg * 64, (gg + 1) * 64)
                bk = pb_rr()
                P.mm(pbk[bk][:, 0:128], Pm[0][pr, j, :, :].rearrange("p s c -> p (s c)"), Qre[pr, j, 0:8, :].rearrange("p s c -> p (s c)"),
                     True, False, PM[0] + QRE, [("ps", bk)])
                P.mm(pbk[bk][:, 0:128], Pm[1][pr, j, :, :].rearrange("p s c -> p (s c)"), Qin[pr, j, 0:8, :].rearrange("p s c -> p (s c)"),
                     False, True, PM[1] + QIN, [("ps", bk)])
                t = g % 2
                P.tt("vector", tz1[t][:], pbk[bk][:, 0:128], MK, ALU.mult, [("ps", bk), "mask"], [f"tz1{t}"])
                dg = bass.AP(dbc.tensor if hasattr(dbc, "tensor") else dbc[:].tensor, dbc[:, g:g + 1, :].offset,
                             [list(dbc[:, g:g + 1, :].ap[0]), [0, 8], [1, 16]])
                P.tt("gpsimd", tz2[t][:].rearrange("p (s c) -> p s c", c=16), identf[:].rearrange("p (s c) -> p s c", c=16), dg, ALU.mult,
                     ["identf", "dbc"] + LD, [f"tz2{t}"])
                P.tt("gpsimd", Toep[:, g, :], tz1[t][:], tz2[t][:], ALU.add, [f"tz1{t}", f"tz2{t}"], ["Toep"])
                P.copy("gpsimd", Cz[0][pr, g, :], Qre[pr, j, 1:9, :].rearrange("p s c -> p (s c)"), QRE + ["Cz0"], ["Cz0"])
                P.copy("gpsimd", Cz[1][pr, g, :], Qin[pr, j, 1:9, :].rearrange("p s c -> p (s c)"), QIN + ["Cz1"], ["Cz1"])
            if "ssm0" in debug:
                P.dma("sync", dbg_out("Wb0", [128, 16, 128], BF16), Wb[0][:], reads=["Wb0"], key="dbg")
                P.dma("sync", dbg_out("Wb1", [128, 16, 128], BF16), Wb[1][:], reads=["Wb1"], key="dbg")
                P.dma("sync", dbg_out("Toep", [128, 32, 128], BF16), Toep[:], reads=["Toep"], key="dbg")
                P.dma("sync", dbg_out("Cz0", [128, 32, 128], BF16), Cz[0][:], reads=["Cz0"], key="dbg")
                P.dma("sync", dbg_out("Cz1", [128, 32, 128], BF16), Cz[1][:], reads=["Cz1"], key="dbg")
                P.dma("sync", dbg_out("th8", [128, 16], F32), th8[:], reads=["th8"], key="dbg")
                P.dma("sync", dbg_out("ar8", [128, 16], F32), ar8[:], reads=["ar8"], key="dbg")
        if stop_after == 2.5:
            sstack.close()
            gstack.close()
            return nc, dbg

        with Phase(nc, "s1", G) as P:
            if pad:
                P.sb("pad", [128, pad], F32)
            U8 = P.sb("U8", [128, 32, NC], BF16)
            for gq in range(4):
                P.dma("sync", U8[:, gq * 8:(gq + 1) * 8, :], u8_s[:, gq * 8:(gq + 1) * 8, :], writes=[("U8", gq)], key=f"U8{gq}")
            wglu = P.sb("wglu", [128, 4, 512], BF16)
            cast_load(P, wglu, w_glu, 4, "wglu", "wglu")
            bglu = P.sb("bglu", [128, 4, 1], F32)
            P.dma("sync", bglu[:], bass.AP(b_glu.tensor, 0, [[1, 128], [128, 4], [1, 1]]), writes=["bglu"], key="bglu", allow_slow_non_contiguous=True)
            identf = P.sb("identf", [128, 128], F32)
            ident = P.sb("ident", [128, 128], BF16)
            make_ident(P, identf, ident)
            X8 = [P.sb(f"X8{i}", [128, 16, NC + 2], BF16) for i in range(2)]
            for i in range(2):
                P.op("gpsimd", "memset", [], [(f"X8{i}", "z")], ap=X8[i][:, :, 0:1], constant=0.0)
            rampi = P.sb("rampi", [128, NC], I32)
            ramp = P.sb("ramp", [128, NC], F32)
            ones = P.sb("onesf", [128, NC], F32)
            P.op("gpsimd", "iota", [], ["rampi"], out=rampi[:], pattern=[[1, NC]], base=0, channel_multiplier=0)
            P.copy("gpsimd", ramp[:], rampi[:], ["rampi"], ["ramp"])
            P.op("gpsimd", "memset", [], ["onesf"], ap=ones[:], constant=1.0)
            ang = P.sb("ang", [128, NC], F32)
            rti = P.sb("rti", [128, NC], I32)
            rtf = P.sb("rtf", [128, NC], F32)
            csn = [P.sb(f"csn{i}", [128, 2, NC], F32) for i in range(2)]
            rho = [P.sb(f"rho{i}", [128, NC], F32) for i in range(2)]
            Ssb = [P.sb(f"Ssb{i}", [128, 2, NC], F32) for i in range(2)]
            w1 = P.sb("w1", [128, NC], F32)
            w2 = P.sb("w2", [128, NC], F32)
            zri = [P.sb(f"zri{i}", [128, 2, NC], F32) for i in range(2)]
            Zri = [P.sb(f"Zri{i}", [128, 2, NC], F32) for i in range(2)]
            q1, q2 = w1, w2
            l1 = [P.ps(f"ps{i}", [128, 512], F32) for i in range(4)]
            yb = [P.ps(f"yb{i}", [128, 512], F32) for i in range(2)]
            tpb = [P.ps(f"tp{i}", [128, 8, 128], BF16) for i in range(2)]
            U8K = [("U8", gq) for gq in range(4)]
            for j in range(16):
                jb = j % 2
                for i in range(2):
                    bk = jb * 2 + i
                    for gg in range(2):
                        g = 2 * j + gg
                        P.mm(l1[bk][gg * 64:(gg + 1) * 64, 0:NC], Wb[i][:, j, gg * 64:(gg + 1) * 64], U8[:, g, :], True, True,
                             [("U8", g // 8)], [("ps", bk)])
                    P.copy("scalar", Ssb[jb][:, i, :], l1[bk][:, 0:NC], [("ps", bk)], [(f"Ssb{jb}", i)])
                P.ts("gpsimd", ang[:], ramp[:], th8[:, j:j + 1], 0.0, ALU.mult, ALU.add, ["ramp"], ["rr_ang"])
                range_reduce_sin(P, "gpsimd", ang[:], csn[jb][:, 1, :], rti[:], rtf[:], "rr", (f"csn{jb}", 1), shift=0.0)
                P.act(ang[:], rtf[:], AF.Abs, ["rr_tf", "rr_ang"], ["rr_ang"])
                P.act(csn[jb][:, 0, :], ang[:], AF.Sin, ["rr_ang"], [(f"csn{jb}", 0)], scale=-1.0, bias=PI / 2)
                P.act(rho[jb][:], ones[:], AF.Exp, ["onesf"], [f"rho{jb}"], scale=ar8[:, j:j + 1])
                CSK = [(f"csn{jb}", 0), (f"csn{jb}", 1)]
                SK = [(f"Ssb{jb}", 0), (f"Ssb{jb}", 1)]
                cs_, sn_ = csn[jb][:, 0, :], csn[jb][:, 1, :]
                sre, sim_ = Ssb[jb][:, 0, :], Ssb[jb][:, 1, :]
                P.tt("vector", w1[:], sre, cs_, ALU.mult, SK + CSK, ["w1"])
                P.tt("vector", w2[:], sim_, sn_, ALU.mult, SK + CSK, ["w2"])
                P.tt("vector", zri[jb][:, 0, :], w1[:], w2[:], ALU.add, ["w1", "w2"], [(f"zri{jb}", 0)])
                P.tt("vector", w1[:], sim_, cs_, ALU.mult, SK + CSK, ["w1"])
                P.tt("vector", w2[:], sre, sn_, ALU.mult, SK + CSK, ["w2"])
                P.tt("vector", zri[jb][:, 1, :], w1[:], w2[:], ALU.subtract, ["w1", "w2"], [(f"zri{jb}", 1)])
                for i in range(2):
                    P.op("vector", "tensor_tensor_scan", [f"rho{jb}", (f"zri{jb}", i)], [(f"Zri{jb}", i)], out=Zri[jb][:, i, :],
                         data0=rho[jb][:], data1=zri[jb][:, i, :], initial=0.0, op0=ALU.mult, op1=ALU.add)
                ZK = [(f"Zri{jb}", 0), (f"Zri{jb}", 1)]
                zre, zim = Zri[jb][:, 0, :], Zri[jb][:, 1, :]
                P.tt("vector", q1[:], zre, cs_, ALU.mult, ZK + CSK, ["w1"])
                P.tt("vector", q2[:], zim, sn_, ALU.mult, ZK + CSK, ["w2"])
                P.tt("vector", X8[0][:, j, 1:NC + 1], q1[:], q2[:], ALU.subtract, ["w1", "w2"], [("X80", j)])
                P.tt("vector", q1[:], zre, sn_, ALU.mult, ZK + CSK, ["w1"])
                P.tt("vector", q2[:], zim, cs_, ALU.mult, ZK + CSK, ["w2"])
                P.tt("vector", X8[1][:, j, 1:NC + 1], q1[:], q2[:], ALU.add, ["w1", "w2"], [("X81", j)])
            zTb = [P.sb(f"zT{i}", [128, 4, 1024], BF16) for i in range(2)]
            sgl = [P.sb(f"sgl{i}", [128, 512], F32) for i in range(2)]
            glo = [P.sb(f"glo{i}", [128, 512], BF16) for i in range(2)]
            z8 = [P.sb(f"z8{i}", [128, 8, 512], BF16) for i in range(2)]
            ysq = [P.sb(f"ysq{i}", [128, 512], F32) for i in range(2)]
            yw = [P.sb(f"yw{i}", [128, 512], F32) for i in range(2)]
            ysg = [P.sb(f"ysg{i}", [128, 512], F32) for i in range(2)]
            yb_rr = RR(range(2))
            ev_rr = RR(["vector", "scalar"])
            for ct in range(NCT):
                zb = ct % 2
                for gq in range(8):
                    b_ = yb_rr()
                    for gi in range(4):
                        g = gq * 4 + gi
                        j = g // 2
                        P.mm(yb[b_][:, gi * 128:(gi + 1) * 128], U8[:, g, ct * 128:(ct + 1) * 128], Toep[:, g, :], True, False,
                             [("U8", g // 8)], [("yb", b_)])
                        P.mm(yb[b_][:, gi * 128:(gi + 1) * 128], X8[0][:, j, ct * 128:(ct + 1) * 128], Cz[0][:, g, :], False, False,
                             [("X80", j), ("X80", "z")], [("yb", b_)])
                        P.mm(yb[b_][:, gi * 128:(gi + 1) * 128], X8[1][:, j, ct * 128:(ct + 1) * 128], Cz[1][:, g, :], False, True,
                             [("X81", j), ("X81", "z")], [("yb", b_)])
                    t = b_
                    P.act(ysq[t][:], yb[b_][:, :], AF.Square, [("yb", b_)], [f"ysq{t}"])
                    P.ts("vector", yw[t][:], ysq[t][:], 0.044715, 1.0, ALU.mult, ALU.add, [f"ysq{t}"], [f"yw{t}"])
                    P.tt("vector", yw[t][:], yw[t][:], yb[b_][:, :], ALU.mult, [f"yw{t}", ("yb", b_)], [f"yw{t}"])
                    P.act(ysg[t][:], yw[t][:], AF.Sigmoid, [f"yw{t}"], [f"ysg{t}"], scale=1.5957691216057308)
                    dst = z8[zb][:, :, gq * 64:(gq + 1) * 64].rearrange("p s (g c) -> p g s c", c=16)
                    P.tt("vector", dst, ysg[t][:].rearrange("p (g s c) -> p g s c", s=8, c=16),
                         yb[b_][:, :].rearrange("p (g s c) -> p g s c", s=8, c=16), ALU.mult, [f"ysg{t}", ("yb", b_)], [(f"z8{zb}", gq)])
                Z8K = [(f"z8{zb}", gq) for gq in range(8)]
                for s_ in range(8):
                    tp, ktp = tpb[s_ % 2], f"tp{s_ % 2}"
                    for m in range(4):
                        P.tr(tp[:, m, :], z8[zb][:, s_, m * 128:(m + 1) * 128], ident[:], Z8K + ["ident"], [ktp])
                    P.copy(ev_rr(), zTb[zb][:, :, s_:1024:8], tp[:, 0:4, :], [ktp], [("zT", zb)])
                for bh in range(2):
                    blk = ct * 2 + bh
                    c0, c1 = blk * 512, (blk + 1) * 512
                    for m in range(4):
                        b_ = yb_rr()
                        for k in range(4):
                            P.mm(yb[b_][:, :], wglu[:, k, m * 128:(m + 1) * 128], zTb[zb][:, k, bh * 512:(bh + 1) * 512], k == 0, k == 3,
                                 [("wglu", k), ("zT", zb)], [("yb", b_)])
                        t = m % 2
                        P.act(sgl[t][:], yb[b_][:, :], AF.Sigmoid, [("yb", b_), "bglu"], [f"sgl{t}"], bias=bglu[:, m, :])
                        P.tt("vector", glo[t][:], zTb[zb][:, m, bh * 512:(bh + 1) * 512], sgl[t][:], ALU.mult, [("zT", zb), f"sgl{t}"], [f"glo{t}"])
                        P.dma("sync", gluT_s[m * 128:(m + 1) * 128, c0:c1], glo[t][:], reads=[f"glo{t}"], key=f"glo{t}")
        sstack.close()

    if "nossm" in debug:
        with Phase(nc, "pz", G) as P:
            zt = P.sb("zt", [128, 512], BF16)
            P.op("gpsimd", "memset", [], ["zt"], ap=zt[:], constant=0.0)
            for blk in range(NB):
                for k in range(4):
                    P.dma("sync", gluT_s[k * 128:(k + 1) * 128, blk * 512:(blk + 1) * 512], zt[:], reads=["zt"], key="z")

    with Phase(nc, "pm", G) as P:
        if pad:
            P.sb("pad", [128, pad], F32)
        wos = P.sb("wos", [128, 4, 1024], BF16)
        wom = P.sb("wom", [128, 8, 1024], BF16)
        wout = P.sb("wout", [128, 8, 1024], BF16)
        cast_load(P, wos, w_o_ssm, 4, "wos", "wos")
        cast_load(P, wom, w_o_mla, 8, "wom", "wom")
        cast_load(P, wout, w_out, 8, "wout", "wout")
        gl = [P.sb(f"gl{i}", [128, 4, 512], BF16) for i in range(2)]
        at = [P.sb(f"at{i}", [128, 8, 512], BF16) for i in range(2)]
        gts = [P.sb(f"gts{i}", [128, 16, 512], BF16) for i in range(2)]
        xs = [P.sb(f"xs{i}", [128, 4, D], F32) for i in range(2)]
        hb = [P.sb(f"hb{i}", [128, 4, D], F32) for i in range(2)]
        mixT = P.sb("mixT", [128, 8, 512], BF16)
        tm1 = [P.sb(f"tm1{i}", [128, 512], F32) for i in range(2)]
        tm2 = [P.sb(f"tm2{i}", [128, 512], F32) for i in range(2)]
        banks = [P.ps(f"b{i}", [128, 512], F32) for i in range(6)]
        bank_rr = RR(range(6))
        for blk in range(NB):
            b2 = blk % 2
            c0, c1 = blk * 512, (blk + 1) * 512
            P.dma("sync", gl[b2][:], gluT_s[:, c0:c1].rearrange("(k p) t -> p k t", p=128), writes=[f"gl{b2}"], key=f"gl{b2}")
            P.dma("sync", at[b2][:], attnT_s[:, c0:c1].rearrange("(k p) t -> p k t", p=128), writes=[f"at{b2}"], key=f"at{b2}")
            P.dma("sync", gts[b2][:], gates_s[:, c0:c1].rearrange("(k p) t -> p k t", p=128), writes=[f"gts{b2}"], key=f"gts{b2}")
            P.dma("sync", xs[b2][:], x[c0:c1, :].rearrange("(j p) d -> p j d", p=128), writes=[f"xs{b2}"], key=f"xs{b2}")
            for m in range(8):
                ba, bb = bank_rr(), bank_rr()
                for k in range(4):
                    P.mm(banks[ba][:, :], wos[:, k, m * 128:(m + 1) * 128], gl[b2][:, k, :], k == 0, k == 3,
                         [("wos", k), f"gl{b2}"], [("bank", ba)])
                for k in range(8):
                    P.mm(banks[bb][:, :], wom[:, k, m * 128:(m + 1) * 128], at[b2][:, k, :], k == 0, k == 7,
                         [("wom", k), f"at{b2}"], [("bank", bb)])
                t = m % 2
                P.tt("vector", tm1[t][:], banks[ba][:, :], gts[b2][:, m, :], ALU.mult, [("bank", ba), f"gts{b2}"], [f"tm1{t}"])
                P.tt("vector", tm2[t][:], banks[bb][:, :], gts[b2][:, 8 + m, :], ALU.mult, [("bank", bb), f"gts{b2}"], [f"tm2{t}"])
                P.tt("gpsimd", mixT[:, m, :], tm1[t][:], tm2[t][:], ALU.add, [f"tm1{t}", f"tm2{t}"], [("mixT", m)])
            for j in range(4):
                for hh in range(2):
                    bc = bank_rr()
                    for k in range(8):
                        P.mm(banks[bc][:, :], mixT[:, k, j * 128:(j + 1) * 128], wout[:, k, hh * 512:(hh + 1) * 512], k == 0, k == 7,
                             [("mixT", k), ("wout", k)], [("bank", bc)])
                    P.tt("vector", hb[b2][:, j, hh * 512:(hh + 1) * 512], banks[bc][:, :], xs[b2][:, j, hh * 512:(hh + 1) * 512], ALU.add,
                         [("bank", bc), f"xs{b2}"], [(f"hb{b2}", j, hh)])
            P.dma("sync", h_s[c0:c1, :].rearrange("(j p) d -> p j d", p=128), hb[b2][:],
                  reads=[(f"hb{b2}", j, hh) for j in range(4) for hh in range(2)], key=f"hb{b2}")
    if stop_after == 3:
        gstack.close()
        return nc, dbg

    with Phase(nc, "pf", G) as P:
        if pad:
            P.sb("pad", [128, pad], F32)
        wup = P.sb("wup", [128, 8, 4096], BF16)
        wdn = P.sb("wdn", [128, 32, 1024], BF16)
        cast_load(P, wup, w_up, 8, "wup", "wup")
        cast_load(P, wdn, w_down, 32, "wdn", "wdn")
        gmlp = P.sb("gmlp", [128, D], F32)
        P.dma("sync", gmlp[:], norm_mlp.partition_broadcast(128), writes=["gmlp"], key="gmlp")
        identf = P.sb("identf", [128, 128], F32)
        ident = P.sb("ident", [128, 128], BF16)
        make_ident(P, identf, ident)
        NB2 = L // 256
        hs = [P.sb(f"hs{i}", [128, 2, D], F32) for i in range(2)]
        ob_ = [P.sb(f"o{i}", [128, 2, D], F32) for i in range(2)]
        hn = [P.sb(f"hn{i}", [128, 2, D], BF16) for i in range(2)]
        hnTb = [P.sb(f"hnT{i}", [128, 8, 256], BF16) for i in range(2)]
        junk = [P.sb(f"junk{i}", [128, D], BF16) for i in range(2)]
        ss = P.sb("ss", [128, 2 * NB2], F32)
        inv = P.sb("inv", [128, 2 * NB2], F32)
        rl = [P.sb(f"rl{i}", [128, 256], F32) for i in range(2)]
        aT = [P.sb(f"aT{i}", [128, 256], BF16) for i in range(3)]
        acc = [P.ps(f"acc{i}", [128, 512], F32) for i in range(4)]
        ub = [P.ps(f"ub{i}", [128, 512], F32) for i in range(2)]
        tpb = [P.ps(f"tp{i}", [128, 8, 128], BF16) for i in range(2)]
        ub_rr, aT_rr, rl_rr = RR(range(2)), RR(range(3)), RR(range(2))
        ev_rr = RR(["vector", "scalar"])
        def ffront(blk):
            b2 = blk % 2
            r0, r1 = blk * 256, (blk + 1) * 256
            P.dma("sync", hs[b2][:], h_s[r0:r1, :].rearrange("(j p) d -> p j d", p=128), writes=[f"hs{b2}"], key=f"hs{b2}")
            for j in range(2):
                col = blk * 2 + j
                P.act(junk[j][:], hs[b2][:, j, :], AF.Square, [f"hs{b2}"], [f"junk{j}", ("ss", blk, j)], accum_out=ss[:, col:col + 1])
            P.act(inv[:, blk * 2:blk * 2 + 2], ss[:, blk * 2:blk * 2 + 2], AF.Sqrt, [("ss", blk, 0), ("ss", blk, 1)], [("inv", blk)],
                  scale=1.0 / D, bias=EPS)
            P.op("vector", "reciprocal", [("inv", blk)], [("inv", blk)], out=inv[:, blk * 2:blk * 2 + 2], in_=inv[:, blk * 2:blk * 2 + 2])
            for j in range(2):
                col = blk * 2 + j
                P.stt(hn[b2][:, j, :], hs[b2][:, j, :], inv[:, col:col + 1], gmlp[:], ALU.mult, ALU.mult,
                      [f"hs{b2}", ("inv", blk), "gmlp"], [(f"hn{b2}", j)])
                tp, ktp = tpb[j], f"tp{j}"
                for k in range(8):
                    P.tr(tp[:, k, :], hn[b2][:, j, k * 128:(k + 1) * 128], ident[:], [(f"hn{b2}", j), "ident"], [ktp])
                P.copy(ev_rr(), hnTb[b2][:, :, j * 128:(j + 1) * 128], tp[:, :, :], [ktp], [f"hnT{b2}"])

        ffront(0)
        for blk in range(NB2):
            b2 = blk % 2
            r0, r1 = blk * 256, (blk + 1) * 256
            hnT = hnTb[b2]

            def up(f):
                u = ub_rr()
                for k in range(8):
                    P.mm(ub[u][:, 0:256], wup[:, k, f * 128:(f + 1) * 128], hnT[:, k, :], k == 0, k == 7, [("wup", k), f"hnT{b2}"], [("ub", u)])
                ri, ai = rl_rr(), aT_rr()
                P.act(rl[ri][:], ub[u][:, 0:256], AF.Relu, [("ub", u)], [f"rl{ri}"])
                P.tt("vector", aT[ai][:], rl[ri][:], rl[ri][:], ALU.mult, [f"rl{ri}"], [f"aT{ai}"])
                return ai

            nxt = up(0)
            for f in range(32):
                ai = nxt
                if f + 1 < 32:
                    nxt = up(f + 1)
                if f == 12 and blk + 1 < NB2:
                    ffront(blk + 1)
                for j in range(2):
                    for hh in range(2):
                        P.mm(acc[j * 2 + hh][:, :], aT[ai][:, j * 128:(j + 1) * 128], wdn[:, f, hh * 512:(hh + 1) * 512], f == 0, f == 31,
                             [f"aT{ai}", ("wdn", f)], [("facc", j * 2 + hh)])
            for j in range(2):
                for hh in range(2):
                    P.tt("vector", ob_[b2][:, j, hh * 512:(hh + 1) * 512], acc[j * 2 + hh][:, :], hs[b2][:, j, hh * 512:(hh + 1) * 512], ALU.add,
                         [("facc", j * 2 + hh), f"hs{b2}"], [(f"o{b2}", j, hh)])
            P.dma("sync", out[r0:r1, :].rearrange("(j p) d -> p j d", p=128), ob_[b2][:],
                  reads=[(f"o{b2}", j, hh) for j in range(2) for hh in range(2)], key=f"o{b2}")
    gstack.close()
    return nc, dbg


def make_in_maps(inputs, L=4096, cores=8):
    half = 32
    inv_freq = (10000.0 ** (-np.arange(half, dtype=np.float32) / half)).astype(np.float32)
    inv_freq2 = np.concatenate([inv_freq, inv_freq]).reshape(64, 1).astype(np.float32)
    maps = []
    for b in range(cores):
        m = {"x": np.ascontiguousarray(inputs["x"][b, :L]), "positions": np.ascontiguousarray(inputs["positions"][b, :L]),
             "inv_freq2": inv_freq2}
        for k, v in inputs.items():
            if k in ("x", "positions"):
                continue
            m[k] = np.ascontiguousarray(np.asarray(v)[0])
        maps.append(m)
    return maps


_NC_CACHE = {}


def kernel(**inputs):
    L = 4096
    if "nc" not in _NC_CACHE:
        _NC_CACHE["nc"] = build(L)[0]
    nc = _NC_CACHE["nc"]
    maps = make_in_maps(inputs, L, 8)
    res = run_bass_kernel_spmd(nc, maps, core_ids=list(range(8)))
    return np.stack([np.asarray(r["out"]) for r in res.results], axis=0).astype(np.float32)
```
